# Optimizing a Trainium2 kernel written in Bass

```python
import jax, jax.numpy as jnp
from jax import lax
import numpy as np

D_MODEL = 2048
BATCH = 4
SEQ = 2048
DEPTH = 4

N_MIXERS = 2
ROPE_THETA = 500000.0
NORM_EPS = 1e-6
BLOCK = 128
POS_OFFSET_MAX = 1024
NEG_INF = -1e30

MLA_HEADS = 16
MLA_Q_RANK = 512
MLA_KV_RANK = 512
MLA_NOPE = 128
MLA_ROPE = 64
MLA_V = 128
MLA_WIDTH = MLA_HEADS * MLA_V
MLA_IN = MLA_Q_RANK + MLA_KV_RANK + MLA_ROPE + MLA_WIDTH

SWA_Q_HEADS = 32
SWA_KV_HEADS = 4
SWA_GROUP = SWA_Q_HEADS // SWA_KV_HEADS
SWA_HEAD_DIM = 64
SWA_WINDOW = 128
SWA_ROPE_DIM = SWA_HEAD_DIM // 4
SWA_WIDTH = SWA_Q_HEADS * SWA_HEAD_DIM
SWA_KV_WIDTH = SWA_KV_HEADS * SWA_HEAD_DIM
SWA_IN = SWA_WIDTH + 2 * SWA_KV_WIDTH + SWA_WIDTH

N_MLA = (DEPTH + 1) // 2
N_SWA = DEPTH // 2

kernel_name = "interleaved_mla_swa_sink_gated_trunk"


def rmsnorm(x, g):
    xf = x.astype(jnp.float32)
    y = xf * lax.rsqrt(jnp.mean(xf * xf, axis=-1, keepdims=True) + NORM_EPS)
    return (y * g.astype(jnp.float32)).astype(x.dtype)


def rope(x, positions, rot_dim):
    half = rot_dim // 2
    inv_freq = ROPE_THETA ** (-jnp.arange(0, rot_dim, 2, dtype=jnp.float32) / rot_dim)
    ang = positions.astype(jnp.float32)[..., None] * inv_freq
    cos = jnp.cos(ang)[:, :, None, :]
    sin = jnp.sin(ang)[:, :, None, :]
    xf = x.astype(jnp.float32)
    x1, x2, rest = xf[..., :half], xf[..., half:rot_dim], xf[..., rot_dim:]
    out = jnp.concatenate([x1 * cos - x2 * sin, x2 * cos + x1 * sin, rest], axis=-1)
    return out.astype(x.dtype)


def mla_causal_attention(q_nope, q_rope, k_nope, k_rope, v):
    B, S, H, _ = q_nope.shape
    nb = S // BLOCK
    scale = (MLA_NOPE + MLA_ROPE) ** -0.5
    key_idx = jnp.arange(S)

    def one_block(i):
        start = i * BLOCK
        qn = lax.dynamic_slice_in_dim(q_nope, start, BLOCK, axis=1)
        qr = lax.dynamic_slice_in_dim(q_rope, start, BLOCK, axis=1)
        s = (jnp.einsum('bqhd,bkhd->bhqk', qn, k_nope)
             + jnp.einsum('bqhr,bkr->bhqk', qr, k_rope)).astype(jnp.float32) * scale
        q_idx = start + jnp.arange(BLOCK)
        causal = key_idx[None, :] <= q_idx[:, None]
        s = jnp.where(causal, s, NEG_INF)
        p = jax.nn.softmax(s, axis=-1).astype(v.dtype)
        return jnp.einsum('bhqk,bkhd->bqhd', p, v)

    out = lax.map(one_block, jnp.arange(nb))
    return out.transpose(1, 0, 2, 3, 4).reshape(B, S, H * MLA_V)


def mla_mixer(h, positions, w_in, q_norm, w_uq, kv_norm, w_ukv):
    B, S, _ = h.shape
    proj = h @ w_in
    c_q, c_kv, k_rope, gate = jnp.split(
        proj, [MLA_Q_RANK, MLA_Q_RANK + MLA_KV_RANK, MLA_Q_RANK + MLA_KV_RANK + MLA_ROPE], axis=-1)
    q = (rmsnorm(c_q, q_norm) @ w_uq).reshape(B, S, MLA_HEADS, MLA_NOPE + MLA_ROPE)
    q_nope = q[..., :MLA_NOPE]
    q_rope = rope(q[..., MLA_NOPE:], positions, MLA_ROPE)
    kv = (rmsnorm(c_kv, kv_norm) @ w_ukv).reshape(B, S, MLA_HEADS, MLA_NOPE + MLA_V)
    k_nope, v = kv[..., :MLA_NOPE], kv[..., MLA_NOPE:]
    k_rope = rope(k_rope[:, :, None, :], positions, MLA_ROPE)[:, :, 0, :]
    y = mla_causal_attention(q_nope, q_rope, k_nope, k_rope, v)
    return y, gate


def swa_sink_attention(q, k, v, sinks):
    B, S, _, D = q.shape
    nb = S // BLOCK
    qb = q.reshape(B, nb, BLOCK, SWA_KV_HEADS, SWA_GROUP, D)

    def band(t):
        tb = t.reshape(B, nb, BLOCK, SWA_KV_HEADS, D)
        prev = jnp.concatenate([jnp.zeros_like(tb[:, :1]), tb[:, :-1]], axis=1)
        return jnp.concatenate([prev, tb], axis=2)

    kw, vw = band(k), band(v)
    s = jnp.einsum('bnqhgd,bnkhd->bnhgqk', qb, kw).astype(jnp.float32) * (D ** -0.5)
    blk = jnp.arange(nb)[:, None, None]
    qi = jnp.arange(BLOCK)[None, :, None]
    kk = jnp.arange(2 * BLOCK)[None, None, :]
    diff = qi + BLOCK - kk
    key_abs = blk * BLOCK - BLOCK + kk
    valid = (diff >= 0) & (diff < SWA_WINDOW) & (key_abs >= 0)
    s = jnp.where(valid[None, :, None, None], s, NEG_INF)
    sink = sinks.astype(jnp.float32).reshape(1, 1, SWA_KV_HEADS, SWA_GROUP, 1, 1)
    m = jnp.maximum(jnp.max(s, axis=-1, keepdims=True), sink)
    e = jnp.exp(s - m)
    p = e / (jnp.sum(e, axis=-1, keepdims=True) + jnp.exp(sink - m))
    o = jnp.einsum('bnhgqk,bnkhd->bnqhgd', p.astype(v.dtype), vw)
    return o.reshape(B, S, SWA_WIDTH)


def swa_mixer(h, positions, w_in, sinks):
    B, S, _ = h.shape
    proj = h @ w_in
    q, k, v, gate = jnp.split(
        proj, [SWA_WIDTH, SWA_WIDTH + SWA_KV_WIDTH, SWA_WIDTH + 2 * SWA_KV_WIDTH], axis=-1)
    q = rope(q.reshape(B, S, SWA_Q_HEADS, SWA_HEAD_DIM), positions, SWA_ROPE_DIM)
    k = rope(k.reshape(B, S, SWA_KV_HEADS, SWA_HEAD_DIM), positions, SWA_ROPE_DIM)
    v = v.reshape(B, S, SWA_KV_HEADS, SWA_HEAD_DIM)
    y = swa_sink_attention(q, k, v, sinks)
    return y, gate


def setup_inputs(seed: int = 0) -> dict:
    key = jax.random.key(seed)
    ks = jax.random.split(key, 16)
    f32 = jnp.float32

    def nrm(k, shape, fan_in):
        return jax.random.normal(k, shape, f32) * fan_in ** -0.5

    def gain(k, shape):
        return 1.0 + 0.05 * jax.random.normal(k, shape, f32)

    x = jax.random.normal(ks[0], (BATCH, SEQ, D_MODEL), f32)
    offset = jax.random.randint(ks[1], (BATCH, 1), 0, POS_OFFSET_MAX, dtype=jnp.int32)
    positions = offset + jnp.arange(SEQ, dtype=jnp.int32)[None, :]
    return {
        "x": x,
        "positions": positions,
        "layer_norm": gain(ks[2], (DEPTH, D_MODEL)),
        "mla_w_in": nrm(ks[3], (N_MLA, D_MODEL, MLA_IN), D_MODEL),
        "mla_q_norm": gain(ks[4], (N_MLA, MLA_Q_RANK)),
        "mla_w_uq": nrm(ks[5], (N_MLA, MLA_Q_RANK, MLA_HEADS * (MLA_NOPE + MLA_ROPE)), MLA_Q_RANK),
        "mla_kv_norm": gain(ks[6], (N_MLA, MLA_KV_RANK)),
        "mla_w_ukv": nrm(ks[7], (N_MLA, MLA_KV_RANK, MLA_HEADS * (MLA_NOPE + MLA_V)), MLA_KV_RANK),
        "mla_w_out": nrm(ks[8], (N_MLA, MLA_WIDTH, D_MODEL), MLA_WIDTH),
        "swa_w_in": nrm(ks[9], (N_SWA, D_MODEL, SWA_IN), D_MODEL),
        "swa_sinks": jax.random.normal(ks[10], (N_SWA, SWA_Q_HEADS), f32),
        "swa_w_out": nrm(ks[11], (N_SWA, SWA_WIDTH, D_MODEL), SWA_WIDTH),
        "final_norm": gain(ks[12], (D_MODEL,)),
    }


def reference(x, positions, layer_norm, mla_w_in, mla_q_norm, mla_w_uq, mla_kv_norm,
              mla_w_ukv, mla_w_out, swa_w_in, swa_sinks, swa_w_out, final_norm):
    for i in range(DEPTH):
        h = rmsnorm(x, layer_norm[i])
        j = i // N_MIXERS
        if i % N_MIXERS == 0:
            y, gate = mla_mixer(h, positions, mla_w_in[j], mla_q_norm[j], mla_w_uq[j],
                                mla_kv_norm[j], mla_w_ukv[j])
            w_out = mla_w_out[j]
        else:
            y, gate = swa_mixer(h, positions, swa_w_in[j], swa_sinks[j])
            w_out = swa_w_out[j]
        x = x + (y * jax.nn.silu(gate)) @ w_out
    return rmsnorm(x, final_norm)
```

```python
import numpy as np
from contextlib import ExitStack
import concourse.bass as bass
import concourse.mybir as mybir
from concourse.bass_utils import run_bass_kernel_spmd

F32 = mybir.dt.float32
BF16 = mybir.dt.bfloat16
I32 = mybir.dt.int32
AF = mybir.ActivationFunctionType
ALU = mybir.AluOpType

S = 2048
D = 2048
NB = 16
DEPTH = 4
EPS = 1e-6
THETA = 500000.0
NEG = -30000.0


class Sem:
    def __init__(self, h, dma):
        self.h = h
        self.issued = 0
        self.dma = dma
        self.sw = False


class Buf:
    def __init__(self, name):
        self.name = name
        self.w = {}
        self.r = {}
        self.dsem = None
        self.persistent = True


class Eng:
    def __init__(self, name, e, sem):
        self.name = name
        self.e = e
        self.sem = sem
        self.seen = {}


def _merge(d, s):
    for k, v in s.items():
        if d.get(k, 0) < v:
            d[k] = v


class K:
    def __init__(self, nc, es):
        self.nc = nc
        self.es = es
        self.engs = {}
        for name, e in (("pe", nc.tensor), ("act", nc.scalar), ("dve", nc.vector),
                        ("pool", nc.gpsimd), ("sp", nc.sync)):
            sem = Sem(es.enter_context(nc.semaphore("s_" + name)), False)
            self.engs[name] = Eng(name, e, sem)
        self.dsems = []
        self.free_sems = []
        self.free_sems_sw = []
        self.bufs = []

    def buf(self, name):
        b = Buf(name)
        self.bufs.append(b)
        return b

    def _wait(self, eng, deps):
        for sem, val in deps.items():
            if sem.dma:
                val = max(val, sem.issued)
            if eng.seen.get(sem, 0) >= val:
                continue
            if sem is eng.sem and val > sem.issued:
                continue
            eng.e.wait_ge(sem.h, val)
            eng.seen[sem] = val

    def op(self, en, fn, reads=(), writes=(), sig=True):
        eng = self.engs[en]
        deps = {}
        for b in reads:
            _merge(deps, b.w)
        for b in writes:
            _merge(deps, b.w)
            _merge(deps, b.r)
        self._wait(eng, deps)
        inst = fn(eng.e)
        sem = eng.sem
        if sig:
            sem.issued += 1
            inst.then_inc(sem.h, 1)
            val = sem.issued
        else:
            val = sem.issued + 1
        for b in reads:
            if b.r.get(sem, 0) < val:
                b.r[sem] = val
        for b in writes:
            b.r = {}
            if b.w.get(sem, 0) < val:
                b.w[sem] = val
        return inst

    def dma(self, qn, out, in_, sb, reads=(), writes=(), par=False):
        eng = self.engs[qn]
        if sb.dsem is None:
            fl = self.free_sems_sw if qn == "pool" else self.free_sems
            if fl:
                sb.dsem = fl.pop()
            else:
                sb.dsem = Sem(self.es.enter_context(self.nc.semaphore("d%d" % len(self.dsems))), True)
                sb.dsem.sw = (qn == "pool")
                self.dsems.append(sb.dsem)
        assert sb.dsem.sw == (qn == "pool"), sb.name
        sem = sb.dsem
        deps = {}
        for b in reads:
            _merge(deps, b.w)
        for b in writes:
            _merge(deps, {s_: v for s_, v in b.w.items() if not s_.dma} if par else b.w)
            _merge(deps, b.r)
        self._wait(eng, deps)
        inst = eng.e.dma_start(out=out, in_=in_)
        sem.issued += 16
        inst.then_inc(sem.h, 16)
        val = sem.issued
        for b in reads:
            b.r[sem] = val
        for b in writes:
            b.r = {}
            b.w[sem] = val
        return inst

    def barrier(self):
        deps = {}
        for e in self.engs.values():
            deps[e.sem] = e.sem.issued
        for s in self.dsems:
            deps[s] = s.issued
        for e in self.engs.values():
            self._wait(e, deps)
        keep = []
        for b in self.bufs:
            b.w = {}
            b.r = {}
            if b.persistent:
                keep.append(b)
            elif b.dsem is not None:
                (self.free_sems_sw if b.dsem.sw else self.free_sems).append(b.dsem)
                b.dsem = None
        self.bufs = keep

    def wait_all(self, en, bufs):
        deps = {}
        for b in bufs:
            _merge(deps, b.w)
        self._wait(self.engs[en], deps)


class Prog:
    def __init__(self, nlayers=DEPTH, debug=False, stop_after=10 ** 9):
        self.nlayers = nlayers
        self.stop_after = stop_after
        self.stage_i = 0
        nc = self.nc = bass.Bass("TRN2", target_bir_lowering=False)
        self.es = ExitStack()
        self.k = K(nc, self.es)
        self.uid = 0
        self.rr = 0
        di = lambda n, s, dt=F32: nc.dram_tensor(n, list(s), dt, kind="ExternalInput").ap()
        self.x = di("x", [S, D])
        self.pos = di("pos", [1, S], I32)
        self.lnorm = di("lnorm", [DEPTH, D])
        self.fnorm = di("fnorm", [1, D])
        self.cst = di("cst", [128, 8])
        self.perm_mla = di("perm_mla", [128, 128])
        self.perm_swa = di("perm_swa", [128, 128])
        self.ident_in = di("ident", [128, 128])
        self.mask_in = di("masks", [128, 256])
        self.m_win_q = di("m_win_q", [2, D, 512])
        self.m_win_kv = di("m_win_kv", [2, D, 512])
        self.m_win_kr = di("m_win_kr", [2, D, 128])
        self.m_win_g = di("m_win_g", [2, D, 2048])
        self.m_qn = di("m_qn", [2, 1, 512])
        self.m_kvn = di("m_kvn", [2, 1, 512])
        self.m_uq_n = di("m_uq_n", [2, 512, 2048])
        self.m_uq_r = di("m_uq_r", [2, 512, 1024])
        self.m_ukv_k = di("m_ukv_k", [2, 512, 2048])
        self.m_ukv_v = di("m_ukv_v", [2, 512, 2048])
        self.m_wout = di("m_wout", [2, D, D])
        self.s_win = di("s_win", [2, D, 4608])
        self.s_sinks = di("s_sinks", [2, 1, 32])
        self.s_wout = di("s_wout", [2, D, D])
        self.y = nc.dram_tensor("y", [S, D], F32, kind="ExternalOutput").ap()
        dt = lambda n, s, d=BF16: nc.dram_tensor(n, list(s), d).ap()
        self.XS = [dt("xs0", [S, D], F32), dt("xs1", [S, D], F32)]
        self.CQ = dt("cq", [512, S])
        self.CKV = dt("ckv", [512, S])
        self.KR = dt("kr", [128, S])
        self.SG = dt("sg", [S, D])
        self.QN = dt("qn", [2048, S])
        self.QR = dt("qr", [1024, S])
        self.KN = dt("kn", [2048, S])
        self.V = dt("v", [S, 2048])
        self.Y = dt("yy", [S, D])
        self.SQ = dt("sq", [2048, S])
        self.SK = dt("sk", [256, S])
        self.SV = dt("sv", [S, 256])
        self.bX = [self.k.buf("bx_in"), self.k.buf("bxs0"), self.k.buf("bxs1")]
        self.bD = {n: self.k.buf("b_" + n) for n in
                   ("cq", "ckv", "kr", "sg", "qn", "qr", "kn", "v", "y", "sq", "sk", "sv", "out")}
        self.build()

    def sb(self, shape, dt=BF16, name=None, st=None):
        self.uid += 1
        n = f"{name or 't'}{self.uid}"
        t = (st or self.es).enter_context(self.nc.sbuf_tensor(n, list(shape), dt))
        b = self.k.buf(n)
        b.persistent = st is None
        return t, b

    def ps(self, shape, dt=F32, st=None):
        self.uid += 1
        n = f"p{self.uid}"
        t = (st or self.es).enter_context(self.nc.psum_tensor(n, list(shape), dt))
        b = self.k.buf(n)
        b.persistent = st is None
        return t, b

    def evac_eng(self):
        self.rr += 1
        return "act" if self.rr % 2 else "dve"

    def copy(self, en, out, in_, reads, writes):
        if en == "act":
            return self.k.op("act", lambda e: e.activation(out=out, in_=in_, func=AF.Identity), reads, writes)
        return self.k.op(en, lambda e: e.tensor_copy(out=out, in_=in_), reads, writes)

    def setup(self):
        k, nc = self.k, self.nc
        self.ident, self.b_ident = self.sb([128, 128], BF16, "ident")
        self.ones, self.b_ones = self.sb([128, 128], BF16, "ones")
        self.pm, self.b_pm = self.sb([128, 2, 128], BF16, "perm")
        self.mask, self.b_mask = self.sb([128, 2, 512], BF16, "mask")
        self.mask01, self.b_mask01 = self.sb([128, 2, 512], BF16, "mask01")
        self.cs, self.b_cs = self.sb([128, 8], F32, "cst")
        self.tab, self.b_tab = self.sb([128, 4, S], F32, "tab")
        self.gam, self.b_gam = self.sb([128, D], F32, "gam")
        k.op("dve", lambda e: e.memset(self.ones[:], 1.0), writes=[self.b_ones])
        self.ones32, self.b_ones32 = self.sb([128, 128], F32, "ones32")
        k.op("dve", lambda e: e.memset(self.ones32[:], 1.0), writes=[self.b_ones32])
        with ExitStack() as st:
            t32, b32 = self.sb([128, 512], F32, "c32", st)
            posi, bpi = self.sb([128, S], I32, "posi", st)
            posf, bpf = self.sb([128, S], F32, "posf", st)
            ang, bang = self.sb([128, S], F32, "ang", st)
            ki, bki = self.sb([128, S], I32, "ki", st)
            kf, bkf = self.sb([128, S], F32, "kf", st)
            mk, bmk = self.sb([128, S], F32, "mk", st)
            for src, dst, n in ((self.ident_in, self.ident[:], 128), (self.perm_mla, self.pm[:, 0, :], 128),
                                (self.perm_swa, self.pm[:, 1, :], 128)):
                k.dma("sp", t32[:, 0:n], src, b32, writes=[b32])
                k.op("dve", lambda e, d=dst, n=n: e.tensor_copy(out=d, in_=t32[:, 0:n]), [b32],
                     [self.b_ident, self.b_pm])
            k.dma("sp", t32[:, 0:256], self.mask_in, b32, writes=[b32])
            for j in range(2):
                for r in range(4):
                    k.op("dve", lambda e, j=j, r=r: e.tensor_copy(out=self.mask[:, j, r * 128:(r + 1) * 128],
                                                                  in_=t32[:, j * 128:(j + 1) * 128]),
                         [b32], [self.b_mask])
            k.op("dve", lambda e: e.tensor_scalar(out=self.mask01[:].rearrange("p a b -> p (a b)"),
                                                  in0=self.mask[:].rearrange("p a b -> p (a b)"), scalar1=0.0, scalar2=None,
                                                  op0=ALU.is_equal), [self.b_mask], [self.b_mask01])
            k.dma("sp", self.cs[:], self.cst, self.b_cs, writes=[self.b_cs])
            k.dma("sp", posi[:], self.pos.partition_broadcast(128), bpi, writes=[bpi])
            k.op("dve", lambda e: e.tensor_copy(out=posf[:], in_=posi[:]), [bpi], [bpf])
            inv2pi = float(1.0 / (2.0 * np.pi))
            for t in range(2):
                for f in range(2):
                    k.op("dve", lambda e, t=t: e.tensor_scalar(out=ang[:], in0=posf[:], scalar1=self.cs[:, 2 * t:2 * t + 1],
                                                               scalar2=None, op0=ALU.mult), [bpf, self.b_cs], [bang])
                    k.op("dve", lambda e, f=f: e.tensor_scalar(out=ang[:], in0=ang[:], scalar1=inv2pi,
                                                               scalar2=0.25 if f == 0 else 0.0, op0=ALU.mult,
                                                               op1=ALU.add), [bang], [bang])
                    k.op("dve", lambda e: e.tensor_copy(out=ki[:], in_=ang[:]), [bang], [bki])
                    k.op("dve", lambda e: e.tensor_copy(out=kf[:], in_=ki[:]), [bki], [bkf])
                    k.op("dve", lambda e: e.tensor_tensor(out=ang[:], in0=ang[:], in1=kf[:], op=ALU.subtract),
                         [bang, bkf], [bang])
                    k.op("dve", lambda e: e.tensor_scalar(out=mk[:], in0=ang[:], scalar1=0.5, scalar2=None,
                                                          op0=ALU.is_gt), [bang], [bmk])
                    k.op("dve", lambda e: e.tensor_tensor(out=ang[:], in0=ang[:], in1=mk[:], op=ALU.subtract),
                         [bang, bmk], [bang])
                    k.op("dve", lambda e: e.tensor_scalar(out=mk[:], in0=ang[:], scalar1=-0.5, scalar2=None,
                                                          op0=ALU.is_lt), [bang], [bmk])
                    k.op("dve", lambda e: e.tensor_tensor(out=ang[:], in0=ang[:], in1=mk[:], op=ALU.add),
                         [bang, bmk], [bang])
                    k.op("dve", lambda e: e.tensor_scalar(out=ang[:], in0=ang[:], scalar1=0.49999, scalar2=-0.49999,
                                                          op0=ALU.min, op1=ALU.max), [bang], [bang])
                    dst = self.tab[:, 2 * t + f, :]
                    k.op("act", lambda e, d=dst: e.activation(out=d, in_=ang[:], func=AF.Sin,
                                                              scale=float(2.0 * np.pi)), [bang], [self.b_tab])
                    if f == 1:
                        k.op("dve", lambda e, d=dst, t=t: e.tensor_scalar(out=d, in0=d, scalar1=self.cs[:, 2 * t + 1:2 * t + 2],
                                                                          scalar2=None, op0=ALU.mult),
                             [self.b_tab, self.b_cs], [self.b_tab])
            k.barrier()

    def norm_stage(self, xsrc, bsrc, grow, HT=None, bHT=None, out=None, bout=None):
        k = self.k
        NBUF = 3
        with ExitStack() as st:
            k.dma("sp", self.gam[:], grow.partition_broadcast(128), self.b_gam, writes=[self.b_gam])
            xt = [self.sb([128, D], F32, "xt", st) for _ in range(NBUF)]
            junk, bj = self.sb([128, D], BF16, "junk", st)
            ss = [self.sb([128, 4], F32, "ss", st) for _ in range(NBUF)]
            if HT is not None:
                hb = [self.sb([128, D], BF16, "hb", st) for _ in range(NBUF)]
                pt = [self.ps([128, 1024], BF16, st) for _ in range(2 * NBUF)]
            else:
                ob = [self.sb([128, D], F32, "ob", st) for _ in range(NBUF)]

            def stage_a(tb):
                x_t, bx = xt[tb % NBUF]
                s_t, bs = ss[tb % NBUF]
                k.dma("sp", x_t[:], xsrc[tb * 128:(tb + 1) * 128, :], bx, reads=[bsrc], writes=[bx])
                k.op("act", lambda e: e.activation(out=junk[:], in_=x_t[:], func=AF.Square, accum_out=s_t[:, 0:1]),
                     [bx], [bj, bs])
                k.op("dve", lambda e: e.tensor_scalar(out=s_t[:, 1:2], in0=s_t[:, 0:1], scalar1=1.0 / D, scalar2=EPS,
                                                      op0=ALU.mult, op1=ALU.add), [bs], [bs])
                k.op("act", lambda e: e.activation(out=s_t[:, 2:3], in_=s_t[:, 1:2], func=AF.Sqrt), [bs], [bs])
                k.op("dve", lambda e: e.reciprocal(out=s_t[:, 3:4], in_=s_t[:, 2:3]), [bs], [bs])
                o_t, bo = (hb if HT is not None else ob)[tb % NBUF]
                k.op("dve", lambda e: e.scalar_tensor_tensor(out=o_t[:], in0=x_t[:], scalar=s_t[:, 3:4], in1=self.gam[:],
                                                             op0=ALU.mult, op1=ALU.mult), [bx, bs, self.b_gam], [bo])

            def stage_b(tb):
                if HT is None:
                    o_t, bo = ob[tb % NBUF]
                    k.dma("sp", out[tb * 128:(tb + 1) * 128, :], o_t[:], bo, reads=[bo], writes=[bout], par=True)
                    return
                h_t, bh = hb[tb % NBUF]
                for half in range(2):
                    p_t, bp = pt[(tb % NBUF) * 2 + half]
                    for c in range(8):
                        cc = half * 8 + c
                        k.op("pe", lambda e, c=c, cc=cc: e.transpose(out=p_t[:, c * 128:(c + 1) * 128],
                                                                    in_=h_t[:, cc * 128:(cc + 1) * 128],
                                                                    identity=self.ident[:]),
                             [bh, self.b_ident], [bp], sig=(c == 7))
                    self.copy("act" if half == 0 else "dve", HT[:, half * 8:(half + 1) * 8, tb * 128:(tb + 1) * 128],
                              p_t[:].rearrange("p (c t) -> p c t", t=128), [bp], [bHT])

            stage_a(0)
            for tb in range(NB):
                if tb + 1 < NB:
                    stage_a(tb + 1)
                stage_b(tb)
            k.barrier()

    def gemm_ctx(self, st, resid=False, c=None):
        class C:
            pass
        if c is None:
            c = C()
            c.wb = [self.sb([128, 8192], BF16, "wb", st) for _ in range(2)]
            if st is not None and resid is None:
                return c
        c.pss = [self.ps([128, 512], F32, st) for _ in range(4)]
        c.pi = c.oi = c.ri = 0
        if resid:
            c.ost = [self.sb([128, 512], F32, "on", st) for _ in range(4)]
            c.xin = [self.sb([128, 512], F32, "xin", st) for _ in range(4)]
            return c
        c.ost = [self.sb([128, 2048], BF16, "ost", st) for _ in range(3)]
        c.natf = [self.sb([128, 512], F32, "natf", st) for _ in range(2)]
        c.natb = [self.sb([128, 512], BF16, "natb", st) for _ in range(2)]
        c.t1 = [self.sb([128, 512], F32, "t1", st) for _ in range(2)]
        c.t2 = [self.sb([128, 512], F32, "t2", st) for _ in range(2)]
        c.ps2 = [self.ps([128, 512], F32, st) for _ in range(2)]
        c.rawf = self.sb([128, 4, 512], F32, "rawf", st)
        c.sqb = self.sb([128, 4, 512], BF16, "sqb", st)
        c.rs = self.sb([128, 512], F32, "rs", st)
        c.pss_s = self.ps([128, 512], F32, st)
        c.gcol = [self.sb([128, 4], F32, "gcol", st) for _ in range(2)]
        c.gi = 0
        return c

    def _wload(self, wslot, W, KC, c0m, MS):
        w_t, bw = wslot
        wv = w_t[:, 0:KC * MS].rearrange("p (c m) -> p c m", m=MS)
        for c0 in range(0, KC, 4):
            c1 = min(KC, c0 + 4)
            self.k.dma("pool", wv[:, c0:c1, :], W[c0 * 128:c1 * 128, c0m:c0m + MS].rearrange("(c p) m -> p c m", p=128),
                       bw, writes=[bw], par=True)
        return wv

    def items_T(self, c, W, KC, M, insb, bin_, mode, dst, bdst, tabi=0, gam=None, pre=None):
        k = self.k
        MS = min(M, 8192 // KC)
        items = []
        for si, ms in enumerate(range(0, M, MS)):
            st8 = {}

            def load(wslot, ms=ms, st8=st8):
                self._wload(wslot, W, KC, ms, MS)

            def mm(wv, bw, mc, tg):
                p_t, bp = c.pss[c.pi % 4]
                c.pi += 1
                tsl = slice(tg * 512, (tg + 1) * 512)
                for kc in range(KC):
                    k.op("pe", lambda e, kc=kc: e.matmul(out=p_t[:], lhsT=wv[:, kc, mc * 128:(mc + 1) * 128],
                                                        rhs=insb[:, kc, tsl], start=(kc == 0), stop=(kc == KC - 1)),
                         [bw, bin_], [bp], sig=(kc == KC - 1))
                return p_t, bp

            def compute(wslot, ms=ms, first=(si == 0), st8=st8):
                if first and pre is not None:
                    pre()
                w_t, bw = wslot
                wv = w_t[:, 0:KC * MS].rearrange("p (c m) -> p c m", m=MS)
                if mode == "latnorm":
                    st8["g"] = c.gcol[c.gi % 2]
                    c.gi += 1
                    gcol, bg = st8["g"]
                    for c_ in range(4):
                        k.dma("sp", gcol[:, c_:c_ + 1], gam[:, c_ * 128:(c_ + 1) * 128].rearrange("o (p q) -> (o p) q", q=1),
                              bg, writes=[bg])
                    rawf, braw = c.rawf
                    sqb, bsq = c.sqb
                    rs, brs = c.rs
                    pss_s, bpss = c.pss_s
                    gcol, bg = st8["g"]
                    for tg in range(4):
                        tsl = slice(tg * 512, (tg + 1) * 512)
                        for mc in range(4):
                            p_t, bp = mm(wv, bw, mc, tg)
                            k.op("act", lambda e, mc=mc: e.activation(out=rawf[:, mc, :], in_=p_t[:], func=AF.Identity),
                                 [bp], [braw])
                            k.op("dve", lambda e, mc=mc: e.tensor_tensor(out=sqb[:, mc, :], in0=rawf[:, mc, :],
                                                                        in1=rawf[:, mc, :], op=ALU.mult), [braw], [bsq])
                        for mc in range(4):
                            k.op("pe", lambda e, mc=mc: e.matmul(out=pss_s[:], lhsT=self.ones[:], rhs=sqb[:, mc, :],
                                                                start=(mc == 0), stop=(mc == 3)),
                                 [bsq, self.b_ones], [bpss], sig=(mc == 3))
                        k.op("dve", lambda e: e.tensor_scalar(out=rs[:], in0=pss_s[:], scalar1=1.0 / 512, scalar2=EPS,
                                                              op0=ALU.mult, op1=ALU.add), [bpss], [brs])
                        k.op("act", lambda e: e.activation(out=rs[:], in_=rs[:], func=AF.Sqrt), [brs], [brs])
                        k.op("dve", lambda e: e.reciprocal(out=rs[:], in_=rs[:]), [brs], [brs])
                        o_t, bo = c.ost[c.oi % 3]
                        c.oi += 1
                        for mc in range(4):
                            k.op("dve", lambda e, mc=mc: e.scalar_tensor_tensor(out=o_t[:, mc * 512:(mc + 1) * 512],
                                                                               in0=rawf[:, mc, :], scalar=gcol[:, mc:mc + 1],
                                                                               in1=rs[:], op0=ALU.mult, op1=ALU.mult),
                                 [braw, brs, bg], [bo])
                        k.dma("sp", dst.rearrange("(c p) t -> p c t", p=128)[:, :, tsl],
                              o_t[:].rearrange("p (c t) -> p c t", t=512), bo, reads=[bo], writes=[bdst], par=True)
                    return
                pend = []

                def rope_tail(nf, bnf, nb_, bnb, o_t, bo, tg):
                    tsl = slice(tg * 512, (tg + 1) * 512)
                    a1, b1 = c.t1[c.ri % 2]
                    a2, b2 = c.t2[c.ri % 2]
                    q_t, bq = c.ps2[c.ri % 2]
                    c.ri += 1
                    k.op("pe", lambda e: e.matmul(out=q_t[:], lhsT=self.pm[:, tabi, :], rhs=nb_[:], start=True, stop=True),
                         [bnb, self.b_pm], [bq])
                    k.op("dve", lambda e: e.tensor_tensor(out=a1[:], in0=nf[:], in1=self.tab[:, 2 * tabi, tsl], op=ALU.mult),
                         [bnf, self.b_tab], [b1])
                    k.op("dve", lambda e: e.tensor_tensor(out=a2[:], in0=q_t[:], in1=self.tab[:, 2 * tabi + 1, tsl],
                                                          op=ALU.mult), [bq, self.b_tab], [b2])
                    k.op("dve", lambda e: e.tensor_tensor(out=o_t[:, tsl], in0=a1[:], in1=a2[:], op=ALU.add), [b1, b2], [bo])

                for mc in range(MS // 128):
                    mg = (ms // 128) + mc
                    o_t, bo = c.ost[c.oi % 3]
                    c.oi += 1
                    for tg in range(4):
                        tsl = slice(tg * 512, (tg + 1) * 512)
                        p_t, bp = mm(wv, bw, mc, tg)
                        if mode == "silu":
                            k.op("act", lambda e: e.activation(out=o_t[:, tsl], in_=p_t[:], func=AF.Silu), [bp], [bo])
                        elif mode == "copy":
                            self.copy(self.evac_eng(), o_t[:, tsl], p_t[:], [bp], [bo])
                        else:
                            slot = (len(pend) + c.ri) % 2
                            nf, bnf = c.natf[c.pi % 2]
                            nb_, bnb = c.natb[c.pi % 2]
                            self.copy("act", nf[:], p_t[:], [bp], [bnf])
                            self.copy("dve", nb_[:], nf[:], [bnf], [bnb])
                            if pend:
                                pend.pop()()
                            pend.append(lambda nf=nf, bnf=bnf, nb_=nb_, bnb=bnb, o_t=o_t, bo=bo, tg=tg, mg=mg:
                                        (rope_tail(nf, bnf, nb_, bnb, o_t, bo, tg),
                                         k.dma("sp", dst[mg * 128:(mg + 1) * 128, :], o_t[:], bo, reads=[bo], writes=[bdst], par=True)
                                         if tg == 3 else None))
                    if mode != "rope":
                        k.dma("sp", dst[mg * 128:(mg + 1) * 128, :], o_t[:], bo, reads=[bo], writes=[bdst], par=True)
                if pend:
                    pend.pop()()

            items.append((load, compute))
        return items

    def items_N(self, c, W, KC, N, insb, bin_, mode, dst, bdst, xsrc=None, bxsrc=None, pre=None):
        k = self.k
        NS = min(N, 8192 // KC)
        items = []
        for si, ns in enumerate(range(0, N, NS)):
            def load(wslot, ns=ns):
                self._wload(wslot, W, KC, ns, NS)

            def compute(wslot, ns=ns, first=(si == 0)):
                if first and pre is not None:
                    pre()
                w_t, bw = wslot
                wv = w_t[:, 0:KC * NS].rearrange("p (c m) -> p c m", m=NS)
                for tb in range(NB):
                    if mode == "resid":
                        o_t, bo = c.ost[c.oi % 4]
                        x_t, bx = c.xin[c.oi % 4]
                        k.dma("act", x_t[:], xsrc[tb * 128:(tb + 1) * 128, ns:ns + NS], bx, reads=[bxsrc], writes=[bx])
                    else:
                        o_t, bo = c.ost[c.oi % 3]
                    c.oi += 1
                    for cg in range(0, NS, 512):
                        cw = min(512, NS - cg)
                        p_t, bp = c.pss[c.pi % 4]
                        c.pi += 1
                        for kc in range(KC):
                            k.op("pe", lambda e, kc=kc: e.matmul(out=p_t[:, 0:cw], lhsT=insb[:, kc, tb * 128:(tb + 1) * 128],
                                                                rhs=wv[:, kc, cg:cg + cw], start=(kc == 0),
                                                                stop=(kc == KC - 1)),
                                 [bw, bin_], [bp], sig=(kc == KC - 1))
                        if mode == "copy":
                            self.copy(self.evac_eng(), o_t[:, cg:cg + cw], p_t[:, 0:cw], [bp], [bo])
                        elif mode == "silu":
                            k.op("act", lambda e: e.activation(out=o_t[:, cg:cg + cw], in_=p_t[:, 0:cw], func=AF.Silu),
                                 [bp], [bo])
                        else:
                            k.op("dve", lambda e: e.tensor_tensor(out=o_t[:, 0:cw], in0=p_t[:, 0:cw], in1=x_t[:, 0:cw],
                                                                  op=ALU.add), [bp, bx], [bo])
                    k.dma("sp", dst[tb * 128:(tb + 1) * 128, ns:ns + NS], o_t[:, 0:NS], bo, reads=[bo], writes=[bdst], par=True)

            items.append((load, compute))
        return items

    def run_items(self, c, items, preloaded=False):
        for i, (load, compute) in enumerate(items):
            if i == 0 and not preloaded:
                load(c.wb[0])
            if i + 1 < len(items):
                items[i + 1][0](c.wb[(i + 1) % 2])
            compute(c.wb[i % 2])

    def load_T(self, dst, bdst, src, bsrc, nchunk):
        for c in range(nchunk):
            self.k.dma("sp", dst[:, c, :], src[c * 128:(c + 1) * 128, :], bdst, reads=[bsrc], writes=[bdst], par=True)

    def mla_attn(self, YG, bYG):
        k = self.k
        scale = float(192 ** -0.5)
        LOOK = 3
        NSB = 4
        DEFER = 3
        pend = []
        with ExitStack() as st:
            krs = [self.sb([128, S], BF16, "kr", st) for _ in range(2)]
            for par, (kr_, bkr_) in enumerate(krs):
                k.dma("sp", kr_[:], self.KR, bkr_, reads=[self.bD["kr"]], writes=[bkr_])
                z0 = 64 if par == 0 else 0
                k.op("dve", lambda e, kr_=kr_, z0=z0: e.memset(kr_[z0:z0 + 64, :], 0.0), [bkr_], [bkr_])
            qn = [self.sb([128, S], BF16, "qn", st) for _ in range(2)]
            kn = [self.sb([128, S], BF16, "kn", st) for _ in range(2)]
            qr = [self.sb([128, S], BF16, "qr", st) for _ in range(2)]
            sg = [self.sb([128, S], BF16, "sgT", st) for _ in range(2)]
            vas = [self.sb([128, NB, 4, 128], BF16, "va", st) for _ in range(2)]
            pT = [self.sb([128, 512], BF16, "pT", st) for _ in range(NSB)]
            rl = [self.sb([128, 512], F32, "rl", st) for _ in range(2)]
            tt = [self.sb([128, 512], F32, "tt", st) for _ in range(2)]
            pS = [self.ps([128, 512], F32, st) for _ in range(NSB)]
            pO = [self.ps([128, 512], F32, st) for _ in range(2)]
            pL = [self.ps([128, 512], F32, st) for _ in range(2)]
            accs = [self.sb([128, 512], F32, "acc", st) for _ in range(2)]

            def load_head(h):
                q_t, bq = qn[h % 2]
                k_t, bk = kn[h % 2]
                g_t, bg = sg[h % 2]
                k.dma("sp", q_t[:], self.QN[h * 128:(h + 1) * 128, :], bq, reads=[self.bD["qn"]], writes=[bq])
                k.dma("sp", k_t[:], self.KN[h * 128:(h + 1) * 128, :], bk, reads=[self.bD["kn"]], writes=[bk])
                if h % 2 == 0:
                    r_t, br = qr[(h // 2) % 2]
                    k.dma("sp", r_t[:], self.QR[(h // 2) * 128:(h // 2 + 1) * 128, :], br, reads=[self.bD["qr"]],
                          writes=[br])

            def load_sg(h):
                g_t, bg = sg[h % 2]
                k.dma("sp", g_t[:], self.SG[h * 128:(h + 1) * 128, :], bg, reads=[self.bD["sg"]], writes=[bg])

            def load_v(hg):
                va, bva = vas[hg % 2]
                for tb in range(NB):
                    k.dma("sp", va[:, tb, :, :],
                          self.V[tb * 128:(tb + 1) * 128, hg * 512:(hg + 1) * 512].rearrange("p (h d) -> p h d", d=128),
                          bva, reads=[self.bD["v"]], writes=[bva], par=True)

            tiles = [(h, qg, kb) for h in range(16) for qg in range(4) for kb in range(qg * 4 + 4)]

            def emit_S(i):
                h, qg, kb = tiles[i]
                if qg == 0 and kb == 0:
                    if h == 0:
                        load_head(0)
                        load_v(0)
                    if h + 1 < 16:
                        load_head(h + 1)
                q_t, bq = qn[h % 2]
                k_t, bk = kn[h % 2]
                r_t, br = qr[(h // 2) % 2]
                r0 = (h % 2) * 64
                qb0 = qg * 4
                j0 = max(0, kb - qb0)
                c0 = qg * 512 + j0 * 128
                c1 = (qg + 1) * 512
                n = c1 - c0
                s_t, bs = pS[i % NSB]
                k.op("pe", lambda e: e.matmul(out=s_t[:, 0:n], lhsT=k_t[:, kb * 128:(kb + 1) * 128],
                                              rhs=q_t[:, c0:c1], start=True, stop=False), [bk, bq], [bs], sig=False)
                if kb >= qb0:
                    k.op("pe", lambda e: e.matmul(out=s_t[:, 0:128], lhsT=self.ident[:], rhs=self.mask[:, 0, 0:128],
                                                  start=False, stop=False), [self.b_ident, self.b_mask], [bs], sig=False)
                kr_, bkr_ = krs[h % 2]
                k.op("pe", lambda e: e.matmul(out=s_t[:, 0:n], lhsT=kr_[:, kb * 128:(kb + 1) * 128],
                                              rhs=r_t[:, c0:c1], start=False, stop=True), [bkr_, br], [bs], sig=True)

            def emit_exp(i):
                h, qg, kb = tiles[i]
                j0 = max(0, kb - qg * 4)
                n = (4 - j0) * 128
                s_t, bs = pS[i % NSB]
                p_t, bp = pT[i % NSB]
                k.op("act", lambda e: e.activation(out=p_t[:, 0:n], in_=s_t[:, 0:n], func=AF.Exp, scale=scale), [bs], [bp])
                while pend and pend[0][0] <= i:
                    pend.pop(0)[1]()

            def emit_pv(i, also=()):
                h, qg, kb = tiles[i]
                hg, hl = h // 4, h % 4
                va, bva = vas[hg % 2]
                qb0 = qg * 4
                j0 = max(0, kb - qb0)
                n = (4 - j0) * 128
                gi = h * 4 + qg
                p_t, bp = pT[i % NSB]
                o_t, bo = pO[gi % 2]
                l_t, bl = pL[gi % 2]
                if qg == 0 and kb == 0:
                    load_sg(h)
                    if hl == 0 and hg + 1 < 4:
                        load_v(hg + 1)
                last = (kb == qb0 + 3)
                k.op("pe", lambda e: e.matmul(out=o_t[:, j0 * 128:512], lhsT=va[:, kb, hl, :], rhs=p_t[:, 0:n],
                                              start=(kb == 0), stop=last), [bp, bva] + list(also), [bo], sig=last)
                a_t, ba = accs[gi % 2]
                if kb == 0:
                    k.op("dve", lambda e: e.tensor_copy(out=a_t[:, 0:512], in_=p_t[:, 0:512]), [bp], [ba])
                else:
                    k.op("dve", lambda e: e.tensor_tensor(out=a_t[:, j0 * 128:512], in0=a_t[:, j0 * 128:512],
                                                          in1=p_t[:, 0:n], op=ALU.add), [bp, ba], [ba])
                if last:
                    r_, brl = rl[gi % 2]
                    t_, bt = tt[gi % 2]
                    g_t, bg = sg[h % 2]

                    def epi(r_=r_, brl=brl, t_=t_, bt=bt, g_t=g_t, bg=bg, o_t=o_t, bo=bo, l_t=l_t, bl=bl, h=h, qg=qg,
                            a_t=a_t, ba=ba):
                        k.op("pe", lambda e: e.matmul(out=l_t[:], lhsT=self.ones32[:], rhs=a_t[:], start=True, stop=True),
                             [ba, self.b_ones32], [bl])
                        k.op("act", lambda e: e.activation(out=r_[:], in_=l_t[:], func=AF.Ln), [bl], [brl])
                        k.op("act", lambda e: e.activation(out=r_[:], in_=r_[:], func=AF.Exp, scale=-1.0), [brl], [brl])
                        k.op("dve", lambda e: e.tensor_tensor(out=t_[:], in0=o_t[:], in1=r_[:], op=ALU.mult), [bo, brl], [bt])
                        k.op("pool", lambda e: e.tensor_tensor(out=YG[:, h, qg * 512:(qg + 1) * 512], in0=t_[:],
                                                               in1=g_t[:, qg * 512:(qg + 1) * 512], op=ALU.mult),
                             [bt, bg], [bYG])
                    pend.append((i + DEFER, epi))

            nt = len(tiles)
            emit_S(0)
            emit_S(1)
            for a in range(0, nt, 2):
                for j in (a + 2, a + 3):
                    if j < nt:
                        emit_S(j)
                emit_exp(a)
                emit_exp(a + 1)
                emit_pv(a, also=[pT[(a + 1) % NSB][1]])
                emit_pv(a + 1)
            while pend:
                pend.pop(0)[1]()
            k.barrier()

    def swa_attn(self, li):
        k = self.k
        self.stage_i += 1
        if self.stage_i > self.stop_after:
            return
        scale = float(64 ** -0.5)
        with ExitStack() as st:
            kT, bkT = self.sb([128, 4, S], BF16, "kT", st)
            for g in range(4):
                k.dma("sp", kT[0:64, g, :], self.SK[g * 64:(g + 1) * 64, :], bkT, reads=[self.bD["sk"]], writes=[bkT], par=True)
            va, bva = self.sb([128, NB, 4, 66], BF16, "sva", st)
            k.op("dve", lambda e: e.memset(va[:].rearrange("p a b c -> p (a b c)"), 1.0), writes=[bva])
            for tb in range(NB):
                k.dma("sp", va[:, tb, :, 0:64], self.SV[tb * 128:(tb + 1) * 128, :].rearrange("p (g d) -> p g d", d=64),
                      bva, writes=[bva], reads=[self.bD["sv"]], par=True)
            es, bes = self.sb([128, 32], F32, "es", st)
            k.dma("sp", es[:], self.s_sinks[li].partition_broadcast(128), bes, writes=[bes])
            k.op("act", lambda e: e.activation(out=es[:], in_=es[:], func=AF.Exp), [bes], [bes])
            qc = [self.sb([128, 4, S], BF16, "qc", st) for _ in range(2)]
            yos = [self.sb([128, NB, 512], BF16, "syo", st) for _ in range(2)]
            pT = [self.sb([128, 512], BF16, "spT", st) for _ in range(6)]
            den = [self.sb([128, 16], F32, "den", st) for _ in range(2)]
            pS = [self.ps([128, 512], F32, st) for _ in range(6)]
            pO = [self.ps([128, 512], F32, st) for _ in range(2)]

            def load_q(G):
                g, cp = G // 2, G % 2
                q_t, bq = qc[G % 2]
                for hh in range(4):
                    hd = g * 8 + cp * 4 + hh
                    k.dma("sp", q_t[0:64, hh, :], self.SQ[hd * 64:(hd + 1) * 64, :], bq, reads=[self.bD["sq"]],
                          writes=[bq], par=True)

            units = [(G, b) for G in range(8) for b in range(NB)]

            def kbs_of(b):
                return [b] if b == 0 else [b - 1, b]

            def emit_S(u):
                G, b = units[u]
                g = G // 2
                if b == 0:
                    if G == 0:
                        load_q(0)
                    if G + 1 < 8:
                        load_q(G + 1)
                q_t, bq = qc[G % 2]
                for ii, kb in enumerate(kbs_of(b)):
                    s_t, bs = pS[(2 * u + ii) % 6]
                    p_t, bp = pT[(2 * u + ii) % 6]
                    mi = 0 if kb == b else 1
                    k.op("pe", lambda e: e.matmul(out=s_t[:, 0:512].rearrange("p (h q) -> p h q", q=128),
                                                  lhsT=kT[0:64, g, kb * 128:(kb + 1) * 128],
                                                  rhs=q_t[0:64, :, b * 128:(b + 1) * 128], start=True, stop=True),
                         [bkT, bq], [bs], sig=True)
                    k.op("act", lambda e: e.activation(out=p_t[:, 0:512], in_=s_t[:, 0:512], func=AF.Exp, scale=scale),
                         [bs], [bp])
                    k.op("dve", lambda e: e.tensor_tensor(out=p_t[:, 0:512], in0=p_t[:, 0:512], in1=self.mask01[:, mi, :],
                                                          op=ALU.mult), [bp, self.b_mask01], [bp])

            def emit_R(u):
                G, b = units[u]
                g, cp = G // 2, G % 2
                yo, byo = yos[g % 2]
                o_t, bo = pO[u % 2]
                d_t, bd = den[u % 2]
                kbs = kbs_of(b)
                allp = [pT[(2 * u + ii) % 6][1] for ii in range(len(kbs))]
                for hh in range(4):
                    for ii, kb in enumerate(kbs):
                        p_t, bp = pT[(2 * u + ii) % 6]
                        k.op("pe", lambda e, hh=hh, p_t=p_t, kb=kb, ii=ii: e.matmul(
                            out=o_t[:, hh * 66:hh * 66 + 65], lhsT=p_t[:, hh * 128:(hh + 1) * 128], rhs=va[:, kb, g, 0:65],
                            start=(ii == 0), stop=(ii == len(kbs) - 1)),
                             ([bva] + allp) if (hh == 0 and ii == 0) else [bp, bva], [bo],
                             sig=(hh == 3 and ii == len(kbs) - 1))
                h0 = g * 8 + cp * 4
                k.op("dve", lambda e: e.tensor_tensor(out=d_t[:, 0:4],
                                                      in0=o_t[:, 0:264].rearrange("p (h d) -> p h d", d=66)[:, :, 64],
                                                      in1=es[:, h0:h0 + 4], op=ALU.add), [bo, bes], [bd])
                k.op("dve", lambda e: e.reciprocal(out=d_t[:, 4:8], in_=d_t[:, 0:4]), [bd], [bd])
                k.op("dve", lambda e: e.tensor_tensor(
                    out=yo[:, b, cp * 256:(cp + 1) * 256].rearrange("p (h d) -> p h d", d=64),
                    in0=o_t[:, 0:264].rearrange("p (h d) -> p h d", d=66)[:, :, 0:64],
                    in1=d_t[:, 4:8].unsqueeze(2).to_broadcast([128, 4, 64]), op=ALU.mult), [bo, bd], [byo])
                if cp == 1 and b == NB - 1:
                    for tb in range(NB):
                        k.dma("sp", self.Y[tb * 128:(tb + 1) * 128, g * 512:(g + 1) * 512], yo[:, tb, :], byo,
                              reads=[byo], writes=[self.bD["y"]], par=True)

            nu = len(units)
            LOOK = 2
            for u in range(min(LOOK, nu)):
                emit_S(u)
            for u in range(nu):
                if u + LOOK < nu:
                    emit_S(u + LOOK)
                emit_R(u)
            k.barrier()

    def gate_T(self, YG, bYG):
        k = self.k
        NBUF = 3
        SPLIT = 10
        with ExitStack() as st:
            yt = [self.sb([128, D], BF16, "yt", st) for _ in range(NBUF)]
            yb2 = []
            for _ in range(NBUF):
                b2 = k.buf("ytb")
                b2.persistent = False
                yb2.append(b2)
            gt = [self.sb([128, D], BF16, "gt", st) for _ in range(NBUF)]
            pt = [self.ps([128, 1024], BF16, st) for _ in range(2 * NBUF)]
            cs = SPLIT * 128

            def stage_a(tb):
                y_t, by = yt[tb % NBUF]
                by2 = yb2[tb % NBUF]
                g_t, bg = gt[tb % NBUF]
                k.dma("sp", y_t[:], self.Y[tb * 128:(tb + 1) * 128, :], by, reads=[self.bD["y"]], writes=[by, by2])
                k.dma("sp", g_t[:], self.SG[tb * 128:(tb + 1) * 128, :], bg, reads=[self.bD["sg"]], writes=[bg])
                k.op("dve", lambda e: e.tensor_tensor(out=y_t[:, 0:cs], in0=y_t[:, 0:cs], in1=g_t[:, 0:cs], op=ALU.mult),
                     [by, bg], [by])
                k.op("pool", lambda e: e.tensor_tensor(out=y_t[:, cs:D], in0=y_t[:, cs:D], in1=g_t[:, cs:D], op=ALU.mult),
                     [by2, bg], [by2])

            def stage_b(tb):
                y_t, by = yt[tb % NBUF]
                by2 = yb2[tb % NBUF]
                for half in range(2):
                    p_t, bp = pt[(tb % NBUF) * 2 + half]
                    for c in range(8):
                        cc = half * 8 + c
                        k.op("pe", lambda e, c=c, cc=cc: e.transpose(out=p_t[:, c * 128:(c + 1) * 128],
                                                                    in_=y_t[:, cc * 128:(cc + 1) * 128],
                                                                    identity=self.ident[:]),
                             [by if cc < SPLIT else by2, self.b_ident], [bp], sig=(c == 7))
                    self.copy("act" if half == 0 else "dve", YG[:, half * 8:(half + 1) * 8, tb * 128:(tb + 1) * 128],
                              p_t[:].rearrange("p (c t) -> p c t", t=128), [bp], [bYG])

            stage_a(0)
            for tb in range(NB):
                if tb + 1 < NB:
                    stage_a(tb + 1)
                stage_b(tb)
            k.barrier()

    def build(self):
        k = self.k
        self.setup()
        xsrc, bsrc = self.x, self.bX[0]
        for li in range(self.nlayers):
            j = li // 2
            xdst, bdst = self.XS[li % 2], self.bX[1 + li % 2]
            with ExitStack() as st:
                HT, bHT = self.sb([128, 16, S], BF16, "HT", st)
                c = self.gemm_ctx(st, resid=None)
                D_ = self.bD
                if li % 2 == 0:
                    c_sb, bc = self.sb([128, 4, S], BF16, "csb", st)
                    it = self.items_T(c, self.m_win_q[j], 16, 512, HT, bHT, "latnorm", self.CQ, D_["cq"], gam=self.m_qn[j])
                    it += self.items_T(c, self.m_win_kv[j], 16, 512, HT, bHT, "latnorm", self.CKV, D_["ckv"], gam=self.m_kvn[j])
                    it += self.items_T(c, self.m_win_kr[j], 16, 128, HT, bHT, "rope", self.KR, D_["kr"], tabi=0)
                    it += self.items_T(c, self.m_win_g[j], 16, 2048, HT, bHT, "silu", self.SG, D_["sg"])
                    it += self.items_T(c, self.m_uq_n[j], 4, 2048, c_sb, bc, "copy", self.QN, D_["qn"],
                                       pre=lambda: self.load_T(c_sb, bc, self.CQ, D_["cq"], 4))
                    it += self.items_T(c, self.m_uq_r[j], 4, 1024, c_sb, bc, "rope", self.QR, D_["qr"], tabi=0)
                    it += self.items_T(c, self.m_ukv_k[j], 4, 2048, c_sb, bc, "copy", self.KN, D_["kn"],
                                       pre=lambda: self.load_T(c_sb, bc, self.CKV, D_["ckv"], 4))
                    it += self.items_N(c, self.m_ukv_v[j], 4, 2048, c_sb, bc, "copy", self.V, D_["v"])
                else:
                    it = self.items_T(c, self.s_win[j][:, 0:2048], 16, 2048, HT, bHT, "rope", self.SQ, D_["sq"], tabi=1)
                    it += self.items_T(c, self.s_win[j][:, 2048:2304], 16, 256, HT, bHT, "rope", self.SK, D_["sk"], tabi=1)
                    it += self.items_N(c, self.s_win[j][:, 2304:2560], 16, 256, HT, bHT, "copy", self.SV, D_["sv"])
                    it += self.items_N(c, self.s_win[j][:, 2560:4608], 16, 2048, HT, bHT, "silu", self.SG, D_["sg"])
                it[0][0](c.wb[0])
                self.norm_stage(xsrc, bsrc, self.lnorm[li:li + 1, :], HT=HT, bHT=bHT)
                self.gemm_ctx(st, c=c)
                self.run_items(c, it, preloaded=True)
                k.barrier()
            with ExitStack() as st:
                YG, bYG = self.sb([128, 16, S], BF16, "YG", st)
                if li % 2 == 0:
                    self.mla_attn(YG, bYG)
                else:
                    self.swa_attn(j)
                    self.gate_T(YG, bYG)
                c = self.gemm_ctx(st, resid=True)
                wout = self.m_wout[j] if li % 2 == 0 else self.s_wout[j]
                self.run_items(c, self.items_N(c, wout, 16, 2048, YG, bYG, "resid", xdst, bdst, xsrc=xsrc, bxsrc=bsrc))
                k.barrier()
            xsrc, bsrc = xdst, bdst
        self.norm_stage(xsrc, bsrc, self.fnorm, out=self.y, bout=self.bD["out"])
        k.wait_all("sp", [self.bD["out"]])
        k.barrier()
        self.es.close()


_PROG = {}


def _consts():
    cst = np.zeros((128, 8), np.float32)
    perm_mla = np.zeros((128, 128), np.float32)
    perm_swa = np.zeros((128, 128), np.float32)
    for i in range(128):
        d = i % 64
        jf = d % 32
        cst[i, 0] = np.float32(THETA) ** np.float32(-(2.0 * jf) / 64.0)
        cst[i, 1] = -1.0 if d < 32 else 1.0
        src = i + 32 if d < 32 else i - 32
        perm_mla[src, i] = 1.0
        if d < 16:
            cst[i, 2] = np.float32(THETA) ** np.float32(-(2.0 * (d % 8)) / 16.0)
            cst[i, 3] = -1.0 if d < 8 else 1.0
            src = i + 8 if d < 8 else i - 8
        else:
            cst[i, 2] = 0.0
            cst[i, 3] = 1.0
            src = i
        perm_swa[src, i] = 1.0
    ident = np.eye(128, dtype=np.float32)
    r = np.arange(128)[:, None]
    c = np.arange(128)[None, :]
    masks = np.zeros((128, 256), np.float32)
    masks[:, 0:128] = np.where(r > c, NEG, 0.0)
    masks[:, 128:256] = np.where(r > c, 0.0, NEG)
    return cst, perm_mla, perm_swa, ident, masks


def _weights(mla_w_in, mla_q_norm, mla_w_uq, mla_kv_norm, mla_w_ukv, mla_w_out, swa_w_in, swa_sinks, swa_w_out):
    c = np.ascontiguousarray
    kr = mla_w_in[:, :, 1024:1088]
    uq = mla_w_uq.reshape(2, 512, 16, 192)
    ukv = mla_w_ukv.reshape(2, 512, 16, 256)
    return {
        "m_win_q": c(mla_w_in[:, :, 0:512]),
        "m_win_kv": c(mla_w_in[:, :, 512:1024]),
        "m_win_kr": c(np.concatenate([kr, kr], axis=2)),
        "m_win_g": c(mla_w_in[:, :, 1088:3136]),
        "m_qn": c(mla_q_norm.reshape(2, 1, 512)),
        "m_kvn": c(mla_kv_norm.reshape(2, 1, 512)),
        "m_uq_n": c(uq[:, :, :, 0:128].reshape(2, 512, 2048)),
        "m_uq_r": c(uq[:, :, :, 128:192].reshape(2, 512, 1024)),
        "m_ukv_k": c(ukv[:, :, :, 0:128].reshape(2, 512, 2048)),
        "m_ukv_v": c(ukv[:, :, :, 128:256].reshape(2, 512, 2048)),
        "m_wout": c(mla_w_out),
        "s_win": c(swa_w_in),
        "s_sinks": c(swa_sinks.reshape(2, 1, 32)),
        "s_wout": c(swa_w_out),
    }


def kernel(x, positions, layer_norm, mla_w_in, mla_q_norm, mla_w_uq, mla_kv_norm, mla_w_ukv, mla_w_out,
           swa_w_in, swa_sinks, swa_w_out, final_norm, _nlayers=DEPTH, _stop=10 ** 9, _cores=8):
    f = lambda a: np.asarray(a, dtype=np.float32)
    x = f(x)
    positions = np.asarray(positions, dtype=np.int32)
    key = (_nlayers, _stop)
    if key not in _PROG:
        _PROG[key] = Prog(_nlayers, stop_after=_stop)
    prog = _PROG[key]
    cst, perm_mla, perm_swa, ident, masks = _consts()
    shared = _weights(f(mla_w_in), f(mla_q_norm), f(mla_w_uq), f(mla_kv_norm), f(mla_w_ukv), f(mla_w_out),
                      f(swa_w_in), f(swa_sinks), f(swa_w_out))
    shared.update({"lnorm": f(layer_norm), "fnorm": f(final_norm).reshape(1, D), "cst": cst, "perm_mla": perm_mla,
                   "perm_swa": perm_swa, "ident": ident, "masks": masks})
    in_maps = []
    for core in range(8):
        m = dict(shared)
        if core % 2 == 0:
            b = core // 2
            m["x"] = np.ascontiguousarray(x[b])
            m["pos"] = np.ascontiguousarray(positions[b].reshape(1, S))
        else:
            m["x"] = np.zeros((S, D), np.float32)
            m["pos"] = np.zeros((1, S), np.int32)
        in_maps.append(m)
    if _cores != 8:
        res = run_bass_kernel_spmd(prog.nc, in_maps[:_cores], core_ids=list(range(_cores)))
        return np.asarray(res.results[0]["y"])[None]
    res = run_bass_kernel_spmd(prog.nc, in_maps, core_ids=list(range(8)))
    out = np.stack([np.asarray(res.results[2 * b]["y"]) for b in range(4)], axis=0)
    return out.astype(np.float32)
```

```python
import numpy as np
from contextlib import ExitStack
import concourse.bass as bass
import concourse.mybir as mybir
from concourse.bass_utils import run_bass_kernel_spmd

F32 = mybir.dt.float32
BF16 = mybir.dt.bfloat16
I32 = mybir.dt.int32
AF = mybir.ActivationFunctionType
ALU = mybir.AluOpType

S = 2048
D = 2048
NB = 16
DEPTH = 4
EPS = 1e-6
THETA = 500000.0
NEG = -30000.0


class Sem:
    def __init__(self, h, dma):
        self.h = h
        self.issued = 0
        self.dma = dma
        self.sw = False


class Buf:
    def __init__(self, name):
        self.name = name
        self.w = {}
        self.r = {}
        self.dsem = None
        self.persistent = True


class Eng:
    def __init__(self, name, e, sem):
        self.name = name
        self.e = e
        self.sem = sem
        self.seen = {}


def _merge(d, s):
    for k, v in s.items():
        if d.get(k, 0) < v:
            d[k] = v


class K:
    def __init__(self, nc, es):
        self.nc = nc
        self.es = es
        self.engs = {}
        for name, e in (("pe", nc.tensor), ("act", nc.scalar), ("dve", nc.vector),
                        ("pool", nc.gpsimd), ("sp", nc.sync)):
            sem = Sem(es.enter_context(nc.semaphore("s_" + name)), False)
            self.engs[name] = Eng(name, e, sem)
        self.dsems = []
        self.free_sems = []
        self.free_sems_sw = []
        self.bufs = []

    def buf(self, name):
        b = Buf(name)
        self.bufs.append(b)
        return b

    def _wait(self, eng, deps):
        for sem, val in deps.items():
            if sem.dma:
                val = max(val, sem.issued)
            if eng.seen.get(sem, 0) >= val:
                continue
            if sem is eng.sem and val > sem.issued:
                continue
            eng.e.wait_ge(sem.h, val)
            eng.seen[sem] = val

    def op(self, en, fn, reads=(), writes=(), sig=True):
        eng = self.engs[en]
        deps = {}
        for b in reads:
            _merge(deps, b.w)
        for b in writes:
            _merge(deps, b.w)
            _merge(deps, b.r)
        self._wait(eng, deps)
        inst = fn(eng.e)
        sem = eng.sem
        if sig:
            sem.issued += 1
            inst.then_inc(sem.h, 1)
            val = sem.issued
        else:
            val = sem.issued + 1
        for b in reads:
            if b.r.get(sem, 0) < val:
                b.r[sem] = val
        for b in writes:
            b.r = {}
            if b.w.get(sem, 0) < val:
                b.w[sem] = val
        return inst

    def dma(self, qn, out, in_, sb, reads=(), writes=(), par=False):
        eng = self.engs[qn]
        if sb.dsem is None:
            fl = self.free_sems_sw if qn == "pool" else self.free_sems
            if fl:
                sb.dsem = fl.pop()
            else:
                sb.dsem = Sem(self.es.enter_context(self.nc.semaphore("d%d" % len(self.dsems))), True)
                sb.dsem.sw = (qn == "pool")
                self.dsems.append(sb.dsem)
        assert sb.dsem.sw == (qn == "pool"), sb.name
        sem = sb.dsem
        deps = {}
        for b in reads:
            _merge(deps, b.w)
        for b in writes:
            _merge(deps, {s_: v for s_, v in b.w.items() if not s_.dma} if par else b.w)
            _merge(deps, b.r)
        self._wait(eng, deps)
        inst = eng.e.dma_start(out=out, in_=in_)
        sem.issued += 16
        inst.then_inc(sem.h, 16)
        val = sem.issued
        for b in reads:
            b.r[sem] = val
        for b in writes:
            b.r = {}
            b.w[sem] = val
        return inst

    def barrier(self):
        deps = {}
        for e in self.engs.values():
            deps[e.sem] = e.sem.issued
        for s in self.dsems:
            deps[s] = s.issued
        for e in self.engs.values():
            self._wait(e, deps)
        keep = []
        for b in self.bufs:
            b.w = {}
            b.r = {}
            if b.persistent:
                keep.append(b)
            elif b.dsem is not None:
                (self.free_sems_sw if b.dsem.sw else self.free_sems).append(b.dsem)
                b.dsem = None
        self.bufs = keep

    def wait_all(self, en, bufs):
        deps = {}
        for b in bufs:
            _merge(deps, b.w)
        self._wait(self.engs[en], deps)


class Prog:
    def __init__(self, nlayers=DEPTH, debug=False, stop_after=10 ** 9):
        self.nlayers = nlayers
        self.stop_after = stop_after
        self.stage_i = 0
        nc = self.nc = bass.Bass("TRN2", target_bir_lowering=False)
        self.es = ExitStack()
        self.k = K(nc, self.es)
        self.uid = 0
        self.rr = 0
        di = lambda n, s, dt=F32: nc.dram_tensor(n, list(s), dt, kind="ExternalInput").ap()
        self.x = di("x", [S, D])
        self.pos = di("pos", [1, S], I32)
        self.lnorm = di("lnorm", [DEPTH, D])
        self.fnorm = di("fnorm", [1, D])
        self.cst = di("cst", [128, 8])
        self.perm_mla = di("perm_mla", [128, 128])
        self.perm_swa = di("perm_swa", [128, 128])
        self.ident_in = di("ident", [128, 128])
        self.mask_in = di("masks", [128, 256])
        self.m_win_q = di("m_win_q", [2, D, 512])
        self.m_win_kv = di("m_win_kv", [2, D, 512])
        self.m_win_kr = di("m_win_kr", [2, D, 128])
        self.m_win_g = di("m_win_g", [2, D, 2048])
        self.m_qn = di("m_qn", [2, 1, 512])
        self.m_kvn = di("m_kvn", [2, 1, 512])
        self.m_uq_n = di("m_uq_n", [2, 512, 2048])
        self.m_uq_r = di("m_uq_r", [2, 512, 1024])
        self.m_ukv_k = di("m_ukv_k", [2, 512, 2048])
        self.m_ukv_v = di("m_ukv_v", [2, 512, 2048])
        self.m_wout = di("m_wout", [2, D, D])
        self.s_win = di("s_win", [2, D, 4608])
        self.s_sinks = di("s_sinks", [2, 1, 32])
        self.s_wout = di("s_wout", [2, D, D])
        self.y = nc.dram_tensor("y", [S, D], F32, kind="ExternalOutput").ap()
        dt = lambda n, s, d=BF16: nc.dram_tensor(n, list(s), d).ap()
        self.XS = [dt("xs0", [S, D], F32), dt("xs1", [S, D], F32)]
        self.CQ = dt("cq", [512, S])
        self.CKV = dt("ckv", [512, S])
        self.KR = dt("kr", [128, S])
        self.SG = dt("sg", [S, D])
        self.QN = dt("qn", [2048, S])
        self.QR = dt("qr", [1024, S])
        self.KN = dt("kn", [2048, S])
        self.V = dt("v", [S, 2048])
        self.Y = dt("yy", [S, D])
        self.SQ = dt("sq", [2048, S])
        self.SK = dt("sk", [256, S])
        self.SV = dt("sv", [S, 256])
        self.bX = [self.k.buf("bx_in"), self.k.buf("bxs0"), self.k.buf("bxs1")]
        self.bD = {n: self.k.buf("b_" + n) for n in
                   ("cq", "ckv", "kr", "sg", "qn", "qr", "kn", "v", "y", "sq", "sk", "sv", "out")}
        self.build()

    def sb(self, shape, dt=BF16, name=None, st=None):
        self.uid += 1
        n = f"{name or 't'}{self.uid}"
        t = (st or self.es).enter_context(self.nc.sbuf_tensor(n, list(shape), dt))
        b = self.k.buf(n)
        b.persistent = st is None
        return t, b

    def ps(self, shape, dt=F32, st=None):
        self.uid += 1
        n = f"p{self.uid}"
        t = (st or self.es).enter_context(self.nc.psum_tensor(n, list(shape), dt))
        b = self.k.buf(n)
        b.persistent = st is None
        return t, b

    def evac_eng(self):
        self.rr += 1
        return "act" if self.rr % 2 else "dve"

    def copy(self, en, out, in_, reads, writes):
        if en == "act":
            return self.k.op("act", lambda e: e.activation(out=out, in_=in_, func=AF.Identity), reads, writes)
        return self.k.op(en, lambda e: e.tensor_copy(out=out, in_=in_), reads, writes)

    def setup(self):
        k, nc = self.k, self.nc
        self.ident, self.b_ident = self.sb([128, 128], BF16, "ident")
        self.ones, self.b_ones = self.sb([128, 128], BF16, "ones")
        self.pm, self.b_pm = self.sb([128, 2, 128], BF16, "perm")
        self.mask, self.b_mask = self.sb([128, 2, 512], BF16, "mask")
        self.mask01, self.b_mask01 = self.sb([128, 2, 512], BF16, "mask01")
        self.cs, self.b_cs = self.sb([128, 8], F32, "cst")
        self.tab, self.b_tab = self.sb([128, 4, S], F32, "tab")
        self.gam, self.b_gam = self.sb([128, D], F32, "gam")
        k.op("dve", lambda e: e.memset(self.ones[:], 1.0), writes=[self.b_ones])
        with ExitStack() as st:
            t32, b32 = self.sb([128, 512], F32, "c32", st)
            posi, bpi = self.sb([128, S], I32, "posi", st)
            posf, bpf = self.sb([128, S], F32, "posf", st)
            ang, bang = self.sb([128, S], F32, "ang", st)
            ki, bki = self.sb([128, S], I32, "ki", st)
            kf, bkf = self.sb([128, S], F32, "kf", st)
            mk, bmk = self.sb([128, S], F32, "mk", st)
            for src, dst, n in ((self.ident_in, self.ident[:], 128), (self.perm_mla, self.pm[:, 0, :], 128),
                                (self.perm_swa, self.pm[:, 1, :], 128)):
                k.dma("sp", t32[:, 0:n], src, b32, writes=[b32])
                k.op("dve", lambda e, d=dst, n=n: e.tensor_copy(out=d, in_=t32[:, 0:n]), [b32],
                     [self.b_ident, self.b_pm])
            k.dma("sp", t32[:, 0:256], self.mask_in, b32, writes=[b32])
            for j in range(2):
                for r in range(4):
                    k.op("dve", lambda e, j=j, r=r: e.tensor_copy(out=self.mask[:, j, r * 128:(r + 1) * 128],
                                                                  in_=t32[:, j * 128:(j + 1) * 128]),
                         [b32], [self.b_mask])
            k.op("dve", lambda e: e.tensor_scalar(out=self.mask01[:].rearrange("p a b -> p (a b)"),
                                                  in0=self.mask[:].rearrange("p a b -> p (a b)"), scalar1=0.0, scalar2=None,
                                                  op0=ALU.is_equal), [self.b_mask], [self.b_mask01])
            k.dma("sp", self.cs[:], self.cst, self.b_cs, writes=[self.b_cs])
            k.dma("sp", posi[:], self.pos.partition_broadcast(128), bpi, writes=[bpi])
            k.op("dve", lambda e: e.tensor_copy(out=posf[:], in_=posi[:]), [bpi], [bpf])
            inv2pi = float(1.0 / (2.0 * np.pi))
            for t in range(2):
                for f in range(2):
                    k.op("dve", lambda e, t=t: e.tensor_scalar(out=ang[:], in0=posf[:], scalar1=self.cs[:, 2 * t:2 * t + 1],
                                                               scalar2=None, op0=ALU.mult), [bpf, self.b_cs], [bang])
                    k.op("dve", lambda e, f=f: e.tensor_scalar(out=ang[:], in0=ang[:], scalar1=inv2pi,
                                                               scalar2=0.25 if f == 0 else 0.0, op0=ALU.mult,
                                                               op1=ALU.add), [bang], [bang])
                    k.op("dve", lambda e: e.tensor_copy(out=ki[:], in_=ang[:]), [bang], [bki])
                    k.op("dve", lambda e: e.tensor_copy(out=kf[:], in_=ki[:]), [bki], [bkf])
                    k.op("dve", lambda e: e.tensor_tensor(out=ang[:], in0=ang[:], in1=kf[:], op=ALU.subtract),
                         [bang, bkf], [bang])
                    k.op("dve", lambda e: e.tensor_scalar(out=mk[:], in0=ang[:], scalar1=0.5, scalar2=None,
                                                          op0=ALU.is_gt), [bang], [bmk])
                    k.op("dve", lambda e: e.tensor_tensor(out=ang[:], in0=ang[:], in1=mk[:], op=ALU.subtract),
                         [bang, bmk], [bang])
                    k.op("dve", lambda e: e.tensor_scalar(out=mk[:], in0=ang[:], scalar1=-0.5, scalar2=None,
                                                          op0=ALU.is_lt), [bang], [bmk])
                    k.op("dve", lambda e: e.tensor_tensor(out=ang[:], in0=ang[:], in1=mk[:], op=ALU.add),
                         [bang, bmk], [bang])
                    k.op("dve", lambda e: e.tensor_scalar(out=ang[:], in0=ang[:], scalar1=0.49999, scalar2=-0.49999,
                                                          op0=ALU.min, op1=ALU.max), [bang], [bang])
                    dst = self.tab[:, 2 * t + f, :]
                    k.op("act", lambda e, d=dst: e.activation(out=d, in_=ang[:], func=AF.Sin,
                                                              scale=float(2.0 * np.pi)), [bang], [self.b_tab])
                    if f == 1:
                        k.op("dve", lambda e, d=dst, t=t: e.tensor_scalar(out=d, in0=d, scalar1=self.cs[:, 2 * t + 1:2 * t + 2],
                                                                          scalar2=None, op0=ALU.mult),
                             [self.b_tab, self.b_cs], [self.b_tab])
            k.barrier()

    def norm_stage(self, xsrc, bsrc, grow, HT=None, bHT=None, out=None, bout=None):
        k = self.k
        NBUF = 3
        with ExitStack() as st:
            k.dma("sp", self.gam[:], grow.partition_broadcast(128), self.b_gam, writes=[self.b_gam])
            xt = [self.sb([128, D], F32, "xt", st) for _ in range(NBUF)]
            junk, bj = self.sb([128, D], BF16, "junk", st)
            ss = [self.sb([128, 4], F32, "ss", st) for _ in range(NBUF)]
            if HT is not None:
                hb = [self.sb([128, D], BF16, "hb", st) for _ in range(NBUF)]
                pt = [self.ps([128, 1024], BF16, st) for _ in range(2 * NBUF)]
            else:
                ob = [self.sb([128, D], F32, "ob", st) for _ in range(NBUF)]

            def stage_a(tb):
                x_t, bx = xt[tb % NBUF]
                s_t, bs = ss[tb % NBUF]
                k.dma("sp", x_t[:], xsrc[tb * 128:(tb + 1) * 128, :], bx, reads=[bsrc], writes=[bx])
                k.op("act", lambda e: e.activation(out=junk[:], in_=x_t[:], func=AF.Square, accum_out=s_t[:, 0:1]),
                     [bx], [bj, bs])
                k.op("dve", lambda e: e.tensor_scalar(out=s_t[:, 1:2], in0=s_t[:, 0:1], scalar1=1.0 / D, scalar2=EPS,
                                                      op0=ALU.mult, op1=ALU.add), [bs], [bs])
                k.op("act", lambda e: e.activation(out=s_t[:, 2:3], in_=s_t[:, 1:2], func=AF.Sqrt), [bs], [bs])
                k.op("dve", lambda e: e.reciprocal(out=s_t[:, 3:4], in_=s_t[:, 2:3]), [bs], [bs])
                o_t, bo = (hb if HT is not None else ob)[tb % NBUF]
                k.op("dve", lambda e: e.scalar_tensor_tensor(out=o_t[:], in0=x_t[:], scalar=s_t[:, 3:4], in1=self.gam[:],
                                                             op0=ALU.mult, op1=ALU.mult), [bx, bs, self.b_gam], [bo])

            def stage_b(tb):
                if HT is None:
                    o_t, bo = ob[tb % NBUF]
                    k.dma("sp", out[tb * 128:(tb + 1) * 128, :], o_t[:], bo, reads=[bo], writes=[bout], par=True)
                    return
                h_t, bh = hb[tb % NBUF]
                for half in range(2):
                    p_t, bp = pt[(tb % NBUF) * 2 + half]
                    for c in range(8):
                        cc = half * 8 + c
                        k.op("pe", lambda e, c=c, cc=cc: e.transpose(out=p_t[:, c * 128:(c + 1) * 128],
                                                                    in_=h_t[:, cc * 128:(cc + 1) * 128],
                                                                    identity=self.ident[:]),
                             [bh, self.b_ident], [bp], sig=(c == 7))
                    self.copy("act" if half == 0 else "dve", HT[:, half * 8:(half + 1) * 8, tb * 128:(tb + 1) * 128],
                              p_t[:].rearrange("p (c t) -> p c t", t=128), [bp], [bHT[half]])

            stage_a(0)
            for tb in range(NB):
                if tb + 1 < NB:
                    stage_a(tb + 1)
                stage_b(tb)
            k.barrier()

    def gemm_ctx(self, st, resid=False, c=None):
        class C:
            pass
        if c is None:
            c = C()
            c.wb = [self.sb([128, 8192], BF16, "wb", st) for _ in range(2)]
            if st is not None and resid is None:
                return c
        c.pss = [self.ps([128, 512], F32, st) for _ in range(4)]
        c.pi = c.oi = c.ri = 0
        if resid:
            c.ost = [self.sb([128, 512], F32, "on", st) for _ in range(4)]
            c.xin = [self.sb([128, 512], F32, "xin", st) for _ in range(4)]
            return c
        c.ost = [self.sb([128, 2048], BF16, "ost", st) for _ in range(3)]
        c.natf = [self.sb([128, 512], F32, "natf", st) for _ in range(2)]
        c.natb = [self.sb([128, 512], BF16, "natb", st) for _ in range(2)]
        c.t1 = [self.sb([128, 512], F32, "t1", st) for _ in range(2)]
        c.t2 = [self.sb([128, 512], F32, "t2", st) for _ in range(2)]
        c.ps2 = [self.ps([128, 512], F32, st) for _ in range(2)]
        c.rawf = self.sb([128, 4, 512], F32, "rawf", st)
        c.sqb = self.sb([128, 4, 512], BF16, "sqb", st)
        c.rs = self.sb([128, 512], F32, "rs", st)
        c.pss_s = self.ps([128, 512], F32, st)
        c.gcol = [self.sb([128, 4], F32, "gcol", st) for _ in range(2)]
        c.gi = 0
        return c

    def _wload(self, wslot, W, KC, c0m, MS):
        w_t, bw = wslot
        wv = w_t[:, 0:KC * MS].rearrange("p (c m) -> p c m", m=MS)
        for c0 in range(0, KC, 4):
            c1 = min(KC, c0 + 4)
            self.k.dma("pool", wv[:, c0:c1, :], W[c0 * 128:c1 * 128, c0m:c0m + MS].rearrange("(c p) m -> p c m", p=128),
                       bw, writes=[bw], par=True)
        return wv

    def items_T(self, c, W, KC, M, insb, bin_, mode, dst, bdst, tabi=0, gam=None, pre=None):
        k = self.k
        bins = list(bin_) if isinstance(bin_, (list, tuple)) else [bin_]
        MS = min(M, 8192 // KC)
        items = []
        for si, ms in enumerate(range(0, M, MS)):
            st8 = {}

            def load(wslot, ms=ms, st8=st8):
                self._wload(wslot, W, KC, ms, MS)

            def mm(wv, bw, mc, tg):
                p_t, bp = c.pss[c.pi % 4]
                c.pi += 1
                tsl = slice(tg * 512, (tg + 1) * 512)
                for kc in range(KC):
                    k.op("pe", lambda e, kc=kc: e.matmul(out=p_t[:], lhsT=wv[:, kc, mc * 128:(mc + 1) * 128],
                                                        rhs=insb[:, kc, tsl], start=(kc == 0), stop=(kc == KC - 1)),
                         [bw] + bins, [bp], sig=(kc == KC - 1))
                return p_t, bp

            def compute(wslot, ms=ms, first=(si == 0), st8=st8):
                if first and pre is not None:
                    pre()
                w_t, bw = wslot
                wv = w_t[:, 0:KC * MS].rearrange("p (c m) -> p c m", m=MS)
                if mode == "latnorm":
                    st8["g"] = c.gcol[c.gi % 2]
                    c.gi += 1
                    gcol, bg = st8["g"]
                    for c_ in range(4):
                        k.dma("sp", gcol[:, c_:c_ + 1], gam[:, c_ * 128:(c_ + 1) * 128].rearrange("o (p q) -> (o p) q", q=1),
                              bg, writes=[bg])
                    rawf, braw = c.rawf
                    sqb, bsq = c.sqb
                    rs, brs = c.rs
                    pss_s, bpss = c.pss_s
                    gcol, bg = st8["g"]
                    for tg in range(4):
                        tsl = slice(tg * 512, (tg + 1) * 512)
                        for mc in range(4):
                            p_t, bp = mm(wv, bw, mc, tg)
                            k.op("act", lambda e, mc=mc: e.activation(out=rawf[:, mc, :], in_=p_t[:], func=AF.Identity),
                                 [bp], [braw])
                            k.op("dve", lambda e, mc=mc: e.tensor_tensor(out=sqb[:, mc, :], in0=rawf[:, mc, :],
                                                                        in1=rawf[:, mc, :], op=ALU.mult), [braw], [bsq])
                        for mc in range(4):
                            k.op("pe", lambda e, mc=mc: e.matmul(out=pss_s[:], lhsT=self.ones[:], rhs=sqb[:, mc, :],
                                                                start=(mc == 0), stop=(mc == 3)),
                                 [bsq, self.b_ones], [bpss], sig=(mc == 3))
                        k.op("dve", lambda e: e.tensor_scalar(out=rs[:], in0=pss_s[:], scalar1=1.0 / 512, scalar2=EPS,
                                                              op0=ALU.mult, op1=ALU.add), [bpss], [brs])
                        k.op("act", lambda e: e.activation(out=rs[:], in_=rs[:], func=AF.Sqrt), [brs], [brs])
                        k.op("dve", lambda e: e.reciprocal(out=rs[:], in_=rs[:]), [brs], [brs])
                        o_t, bo = c.ost[c.oi % 3]
                        c.oi += 1
                        for mc in range(4):
                            k.op("dve", lambda e, mc=mc: e.scalar_tensor_tensor(out=o_t[:, mc * 512:(mc + 1) * 512],
                                                                               in0=rawf[:, mc, :], scalar=gcol[:, mc:mc + 1],
                                                                               in1=rs[:], op0=ALU.mult, op1=ALU.mult),
                                 [braw, brs, bg], [bo])
                        k.dma("sp", dst.rearrange("(c p) t -> p c t", p=128)[:, :, tsl],
                              o_t[:].rearrange("p (c t) -> p c t", t=512), bo, reads=[bo], writes=[bdst], par=True)
                    return
                pend = []

                def rope_tail(nf, bnf, nb_, bnb, o_t, bo, tg):
                    tsl = slice(tg * 512, (tg + 1) * 512)
                    a1, b1 = c.t1[c.ri % 2]
                    a2, b2 = c.t2[c.ri % 2]
                    q_t, bq = c.ps2[c.ri % 2]
                    c.ri += 1
                    k.op("pe", lambda e: e.matmul(out=q_t[:], lhsT=self.pm[:, tabi, :], rhs=nb_[:], start=True, stop=True),
                         [bnb, self.b_pm], [bq])
                    k.op("dve", lambda e: e.tensor_tensor(out=a1[:], in0=nf[:], in1=self.tab[:, 2 * tabi, tsl], op=ALU.mult),
                         [bnf, self.b_tab], [b1])
                    k.op("dve", lambda e: e.tensor_tensor(out=a2[:], in0=q_t[:], in1=self.tab[:, 2 * tabi + 1, tsl],
                                                          op=ALU.mult), [bq, self.b_tab], [b2])
                    k.op("dve", lambda e: e.tensor_tensor(out=o_t[:, tsl], in0=a1[:], in1=a2[:], op=ALU.add), [b1, b2], [bo])

                for mc in range(MS // 128):
                    mg = (ms // 128) + mc
                    o_t, bo = c.ost[c.oi % 3]
                    c.oi += 1
                    for tg in range(4):
                        tsl = slice(tg * 512, (tg + 1) * 512)
                        p_t, bp = mm(wv, bw, mc, tg)
                        if mode == "silu":
                            k.op("act", lambda e: e.activation(out=o_t[:, tsl], in_=p_t[:], func=AF.Silu), [bp], [bo])
                        elif mode == "copy":
                            self.copy(self.evac_eng(), o_t[:, tsl], p_t[:], [bp], [bo])
                        else:
                            slot = (len(pend) + c.ri) % 2
                            nf, bnf = c.natf[c.pi % 2]
                            nb_, bnb = c.natb[c.pi % 2]
                            self.copy("act", nf[:], p_t[:], [bp], [bnf])
                            self.copy("dve", nb_[:], nf[:], [bnf], [bnb])
                            if pend:
                                pend.pop()()
                            pend.append(lambda nf=nf, bnf=bnf, nb_=nb_, bnb=bnb, o_t=o_t, bo=bo, tg=tg, mg=mg:
                                        (rope_tail(nf, bnf, nb_, bnb, o_t, bo, tg),
                                         k.dma("sp", dst[mg * 128:(mg + 1) * 128, :], o_t[:], bo, reads=[bo], writes=[bdst], par=True)
                                         if tg == 3 else None))
                    if mode != "rope":
                        k.dma("sp", dst[mg * 128:(mg + 1) * 128, :], o_t[:], bo, reads=[bo], writes=[bdst], par=True)
                if pend:
                    pend.pop()()

            items.append((load, compute))
        return items

    def items_N(self, c, W, KC, N, insb, bin_, mode, dst, bdst, xsrc=None, bxsrc=None, pre=None):
        k = self.k
        bins = list(bin_) if isinstance(bin_, (list, tuple)) else [bin_]
        NS = min(N, 8192 // KC)
        items = []
        for si, ns in enumerate(range(0, N, NS)):
            def load(wslot, ns=ns):
                self._wload(wslot, W, KC, ns, NS)

            def compute(wslot, ns=ns, first=(si == 0)):
                if first and pre is not None:
                    pre()
                w_t, bw = wslot
                wv = w_t[:, 0:KC * NS].rearrange("p (c m) -> p c m", m=NS)
                for tb in range(NB):
                    if mode == "resid":
                        o_t, bo = c.ost[c.oi % 4]
                        x_t, bx = c.xin[c.oi % 4]
                        k.dma("act", x_t[:], xsrc[tb * 128:(tb + 1) * 128, ns:ns + NS], bx, reads=[bxsrc], writes=[bx])
                    else:
                        o_t, bo = c.ost[c.oi % 3]
                    c.oi += 1
                    for cg in range(0, NS, 512):
                        cw = min(512, NS - cg)
                        p_t, bp = c.pss[c.pi % 4]
                        c.pi += 1
                        for kc in range(KC):
                            k.op("pe", lambda e, kc=kc: e.matmul(out=p_t[:, 0:cw], lhsT=insb[:, kc, tb * 128:(tb + 1) * 128],
                                                                rhs=wv[:, kc, cg:cg + cw], start=(kc == 0),
                                                                stop=(kc == KC - 1)),
                                 [bw] + bins, [bp], sig=(kc == KC - 1))
                        if mode == "copy":
                            self.copy(self.evac_eng(), o_t[:, cg:cg + cw], p_t[:, 0:cw], [bp], [bo])
                        elif mode == "silu":
                            k.op("act", lambda e: e.activation(out=o_t[:, cg:cg + cw], in_=p_t[:, 0:cw], func=AF.Silu),
                                 [bp], [bo])
                        else:
                            k.op("dve", lambda e: e.tensor_tensor(out=o_t[:, 0:cw], in0=p_t[:, 0:cw], in1=x_t[:, 0:cw],
                                                                  op=ALU.add), [bp, bx], [bo])
                    k.dma("sp", dst[tb * 128:(tb + 1) * 128, ns:ns + NS], o_t[:, 0:NS], bo, reads=[bo], writes=[bdst], par=True)

            items.append((load, compute))
        return items

    def run_items(self, c, items, preloaded=False):
        for i, (load, compute) in enumerate(items):
            if i == 0 and not preloaded:
                load(c.wb[0])
            if i + 1 < len(items):
                items[i + 1][0](c.wb[(i + 1) % 2])
            compute(c.wb[i % 2])

    def load_T(self, dst, bdst, src, bsrc, nchunk):
        for c in range(nchunk):
            self.k.dma("sp", dst[:, c, :], src[c * 128:(c + 1) * 128, :], bdst, reads=[bsrc], writes=[bdst], par=True)

    def mla_attn(self, YG, bYG):
        k = self.k
        scale = float(192 ** -0.5)
        LOOK = 3
        NSB = 4
        DEFER = 3
        pend = []
        with ExitStack() as st:
            krs = [self.sb([128, S], BF16, "kr", st) for _ in range(2)]
            for par, (kr_, bkr_) in enumerate(krs):
                k.dma("sp", kr_[:], self.KR, bkr_, reads=[self.bD["kr"]], writes=[bkr_])
                z0 = 64 if par == 0 else 0
                k.op("dve", lambda e, kr_=kr_, z0=z0: e.memset(kr_[z0:z0 + 64, :], 0.0), [bkr_], [bkr_])
            qn = [self.sb([128, S], BF16, "qn", st) for _ in range(2)]
            kn = [self.sb([128, S], BF16, "kn", st) for _ in range(2)]
            qr = [self.sb([128, S], BF16, "qr", st) for _ in range(2)]
            sg = [self.sb([128, S], BF16, "sgT", st) for _ in range(2)]
            vas = [self.sb([128, NB, 4, 128], BF16, "va", st) for _ in range(2)]
            pT = [self.sb([128, 512], BF16, "pT", st) for _ in range(NSB)]
            rl = [self.sb([128, 512], F32, "rl", st) for _ in range(2)]
            tt = [self.sb([128, 512], F32, "tt", st) for _ in range(2)]
            pS = [self.ps([128, 512], F32, st) for _ in range(NSB)]
            pO = [self.ps([128, 512], F32, st) for _ in range(2)]
            pL = [self.ps([128, 512], F32, st) for _ in range(2)]

            def load_head(h):
                q_t, bq = qn[h % 2]
                k_t, bk = kn[h % 2]
                g_t, bg = sg[h % 2]
                k.dma("sp", q_t[:], self.QN[h * 128:(h + 1) * 128, :], bq, reads=[self.bD["qn"]], writes=[bq])
                k.dma("sp", k_t[:], self.KN[h * 128:(h + 1) * 128, :], bk, reads=[self.bD["kn"]], writes=[bk])
                if h % 2 == 0:
                    r_t, br = qr[(h // 2) % 2]
                    k.dma("sp", r_t[:], self.QR[(h // 2) * 128:(h // 2 + 1) * 128, :], br, reads=[self.bD["qr"]],
                          writes=[br])

            def load_sg(h):
                g_t, bg = sg[h % 2]
                k.dma("sp", g_t[:], self.SG[h * 128:(h + 1) * 128, :], bg, reads=[self.bD["sg"]], writes=[bg])

            def load_v(hg):
                va, bva = vas[hg % 2]
                for tb in range(NB):
                    k.dma("sp", va[:, tb, :, :],
                          self.V[tb * 128:(tb + 1) * 128, hg * 512:(hg + 1) * 512].rearrange("p (h d) -> p h d", d=128),
                          bva, reads=[self.bD["v"]], writes=[bva], par=True)

            tiles = [(h, qg, kb) for h in range(16) for qg in range(4) for kb in range(qg * 4 + 4)]

            def emit_S(i):
                h, qg, kb = tiles[i]
                if qg == 0 and kb == 0:
                    if h == 0:
                        load_head(0)
                        load_v(0)
                    if h + 1 < 16:
                        load_head(h + 1)
                q_t, bq = qn[h % 2]
                k_t, bk = kn[h % 2]
                r_t, br = qr[(h // 2) % 2]
                r0 = (h % 2) * 64
                qb0 = qg * 4
                j0 = max(0, kb - qb0)
                c0 = qg * 512 + j0 * 128
                c1 = (qg + 1) * 512
                n = c1 - c0
                s_t, bs = pS[i % NSB]
                k.op("pe", lambda e: e.matmul(out=s_t[:, 0:n], lhsT=k_t[:, kb * 128:(kb + 1) * 128],
                                              rhs=q_t[:, c0:c1], start=True, stop=False), [bk, bq], [bs], sig=False)
                if kb >= qb0:
                    k.op("pe", lambda e: e.matmul(out=s_t[:, 0:128], lhsT=self.ident[:], rhs=self.mask[:, 0, 0:128],
                                                  start=False, stop=False), [self.b_ident, self.b_mask], [bs], sig=False)
                kr_, bkr_ = krs[h % 2]
                k.op("pe", lambda e: e.matmul(out=s_t[:, 0:n], lhsT=kr_[:, kb * 128:(kb + 1) * 128],
                                              rhs=r_t[:, c0:c1], start=False, stop=True), [bkr_, br], [bs], sig=True)

            def emit_exp(i):
                h, qg, kb = tiles[i]
                j0 = max(0, kb - qg * 4)
                n = (4 - j0) * 128
                s_t, bs = pS[i % NSB]
                p_t, bp = pT[i % NSB]
                k.op("act", lambda e: e.activation(out=p_t[:, 0:n], in_=s_t[:, 0:n], func=AF.Exp, scale=scale), [bs], [bp])
                while pend and pend[0][0] <= i:
                    pend.pop(0)[1]()

            def emit_pv(i, also=()):
                h, qg, kb = tiles[i]
                hg, hl = h // 4, h % 4
                va, bva = vas[hg % 2]
                qb0 = qg * 4
                j0 = max(0, kb - qb0)
                n = (4 - j0) * 128
                gi = h * 4 + qg
                p_t, bp = pT[i % NSB]
                o_t, bo = pO[gi % 2]
                l_t, bl = pL[gi % 2]
                if qg == 0 and kb == 0:
                    load_sg(h)
                    if hl == 0 and hg + 1 < 4:
                        load_v(hg + 1)
                last = (kb == qb0 + 3)
                k.op("pe", lambda e: e.matmul(out=o_t[:, j0 * 128:512], lhsT=va[:, kb, hl, :], rhs=p_t[:, 0:n],
                                              start=(kb == 0), stop=last), [bp, bva] + list(also), [bo], sig=last)
                k.op("pe", lambda e: e.matmul(out=l_t[:, j0 * 128:512], lhsT=self.ones[:], rhs=p_t[:, 0:n],
                                              start=(kb == 0), stop=last), [bp, self.b_ones], [bl], sig=last)
                if last:
                    r_, brl = rl[gi % 2]
                    t_, bt = tt[gi % 2]
                    g_t, bg = sg[h % 2]

                    def epi(r_=r_, brl=brl, t_=t_, bt=bt, g_t=g_t, bg=bg, o_t=o_t, bo=bo, l_t=l_t, bl=bl, h=h, qg=qg):
                        k.op("act", lambda e: e.activation(out=r_[:], in_=l_t[:], func=AF.Ln), [bl], [brl])
                        k.op("act", lambda e: e.activation(out=r_[:], in_=r_[:], func=AF.Exp, scale=-1.0), [brl], [brl])
                        k.op("dve", lambda e: e.tensor_tensor(out=t_[:], in0=o_t[:], in1=r_[:], op=ALU.mult), [bo, brl], [bt])
                        k.op("pool", lambda e: e.tensor_tensor(out=YG[:, h, qg * 512:(qg + 1) * 512], in0=t_[:],
                                                               in1=g_t[:, qg * 512:(qg + 1) * 512], op=ALU.mult),
                             [bt, bg], [bYG])
                    pend.append((i + DEFER, epi))

            nt = len(tiles)
            emit_S(0)
            emit_S(1)
            for a in range(0, nt, 2):
                for j in (a + 2, a + 3):
                    if j < nt:
                        emit_S(j)
                emit_exp(a)
                emit_exp(a + 1)
                emit_pv(a, also=[pT[(a + 1) % NSB][1]])
                emit_pv(a + 1)
            while pend:
                pend.pop(0)[1]()
            k.barrier()

    def swa_attn(self, li):
        k = self.k
        self.stage_i += 1
        if self.stage_i > self.stop_after:
            return
        scale = float(64 ** -0.5)
        with ExitStack() as st:
            kT, bkT = self.sb([128, 4, S], BF16, "kT", st)
            for g in range(4):
                k.dma("sp", kT[0:64, g, :], self.SK[g * 64:(g + 1) * 64, :], bkT, reads=[self.bD["sk"]], writes=[bkT], par=True)
            va, bva = self.sb([128, NB, 4, 66], BF16, "sva", st)
            k.op("dve", lambda e: e.memset(va[:].rearrange("p a b c -> p (a b c)"), 1.0), writes=[bva])
            for tb in range(NB):
                k.dma("sp", va[:, tb, :, 0:64], self.SV[tb * 128:(tb + 1) * 128, :].rearrange("p (g d) -> p g d", d=64),
                      bva, writes=[bva], reads=[self.bD["sv"]], par=True)
            es, bes = self.sb([128, 32], F32, "es", st)
            k.dma("sp", es[:], self.s_sinks[li].partition_broadcast(128), bes, writes=[bes])
            k.op("act", lambda e: e.activation(out=es[:], in_=es[:], func=AF.Exp), [bes], [bes])
            qc = [self.sb([128, 4, S], BF16, "qc", st) for _ in range(2)]
            yos = [self.sb([128, NB, 512], BF16, "syo", st) for _ in range(2)]
            pT = [self.sb([128, 512], BF16, "spT", st) for _ in range(6)]
            den = [self.sb([128, 16], F32, "den", st) for _ in range(2)]
            pS = [self.ps([128, 512], F32, st) for _ in range(6)]
            pO = [self.ps([128, 512], F32, st) for _ in range(2)]

            def load_q(G):
                g, cp = G // 2, G % 2
                q_t, bq = qc[G % 2]
                for hh in range(4):
                    hd = g * 8 + cp * 4 + hh
                    k.dma("sp", q_t[0:64, hh, :], self.SQ[hd * 64:(hd + 1) * 64, :], bq, reads=[self.bD["sq"]],
                          writes=[bq], par=True)

            units = [(G, b) for G in range(8) for b in range(NB)]

            def kbs_of(b):
                return [b] if b == 0 else [b - 1, b]

            def emit_S(u):
                G, b = units[u]
                g = G // 2
                if b == 0:
                    if G == 0:
                        load_q(0)
                    if G + 1 < 8:
                        load_q(G + 1)
                q_t, bq = qc[G % 2]
                for ii, kb in enumerate(kbs_of(b)):
                    s_t, bs = pS[(2 * u + ii) % 6]
                    p_t, bp = pT[(2 * u + ii) % 6]
                    mi = 0 if kb == b else 1
                    k.op("pe", lambda e: e.matmul(out=s_t[:, 0:512].rearrange("p (h q) -> p h q", q=128),
                                                  lhsT=kT[0:64, g, kb * 128:(kb + 1) * 128],
                                                  rhs=q_t[0:64, :, b * 128:(b + 1) * 128], start=True, stop=True),
                         [bkT, bq], [bs], sig=True)
                    k.op("act", lambda e: e.activation(out=p_t[:, 0:512], in_=s_t[:, 0:512], func=AF.Exp, scale=scale),
                         [bs], [bp])
                    k.op("dve", lambda e: e.tensor_tensor(out=p_t[:, 0:512], in0=p_t[:, 0:512], in1=self.mask01[:, mi, :],
                                                          op=ALU.mult), [bp, self.b_mask01], [bp])

            def emit_R(u):
                G, b = units[u]
                g, cp = G // 2, G % 2
                yo, byo = yos[g % 2]
                o_t, bo = pO[u % 2]
                d_t, bd = den[u % 2]
                kbs = kbs_of(b)
                allp = [pT[(2 * u + ii) % 6][1] for ii in range(len(kbs))]
                for hh in range(4):
                    for ii, kb in enumerate(kbs):
                        p_t, bp = pT[(2 * u + ii) % 6]
                        k.op("pe", lambda e, hh=hh, p_t=p_t, kb=kb, ii=ii: e.matmul(
                            out=o_t[:, hh * 66:hh * 66 + 65], lhsT=p_t[:, hh * 128:(hh + 1) * 128], rhs=va[:, kb, g, 0:65],
                            start=(ii == 0), stop=(ii == len(kbs) - 1)),
                             ([bva] + allp) if (hh == 0 and ii == 0) else [bp, bva], [bo],
                             sig=(hh == 3 and ii == len(kbs) - 1))
                h0 = g * 8 + cp * 4
                k.op("dve", lambda e: e.tensor_tensor(out=d_t[:, 0:4],
                                                      in0=o_t[:, 0:264].rearrange("p (h d) -> p h d", d=66)[:, :, 64],
                                                      in1=es[:, h0:h0 + 4], op=ALU.add), [bo, bes], [bd])
                k.op("dve", lambda e: e.reciprocal(out=d_t[:, 4:8], in_=d_t[:, 0:4]), [bd], [bd])
                k.op("dve", lambda e: e.tensor_tensor(
                    out=yo[:, b, cp * 256:(cp + 1) * 256].rearrange("p (h d) -> p h d", d=64),
                    in0=o_t[:, 0:264].rearrange("p (h d) -> p h d", d=66)[:, :, 0:64],
                    in1=d_t[:, 4:8].unsqueeze(2).to_broadcast([128, 4, 64]), op=ALU.mult), [bo, bd], [byo])
                if cp == 1 and b == NB - 1:
                    for tb in range(NB):
                        k.dma("sp", self.Y[tb * 128:(tb + 1) * 128, g * 512:(g + 1) * 512], yo[:, tb, :], byo,
                              reads=[byo], writes=[self.bD["y"]], par=True)

            nu = len(units)
            LOOK = 2
            for u in range(min(LOOK, nu)):
                emit_S(u)
            for u in range(nu):
                if u + LOOK < nu:
                    emit_S(u + LOOK)
                emit_R(u)
            k.barrier()

    def gate_T(self, YG, bYG):
        k = self.k
        NBUF = 3
        SPLIT = 10
        with ExitStack() as st:
            yt = [self.sb([128, D], BF16, "yt", st) for _ in range(NBUF)]
            yb2 = []
            for _ in range(NBUF):
                b2 = k.buf("ytb")
                b2.persistent = False
                yb2.append(b2)
            gt = [self.sb([128, D], BF16, "gt", st) for _ in range(NBUF)]
            pt = [self.ps([128, 1024], BF16, st) for _ in range(2 * NBUF)]
            cs = SPLIT * 128

            def stage_a(tb):
                y_t, by = yt[tb % NBUF]
                by2 = yb2[tb % NBUF]
                g_t, bg = gt[tb % NBUF]
                k.dma("sp", y_t[:], self.Y[tb * 128:(tb + 1) * 128, :], by, reads=[self.bD["y"]], writes=[by, by2])
                k.dma("sp", g_t[:], self.SG[tb * 128:(tb + 1) * 128, :], bg, reads=[self.bD["sg"]], writes=[bg])
                k.op("dve", lambda e: e.tensor_tensor(out=y_t[:, 0:cs], in0=y_t[:, 0:cs], in1=g_t[:, 0:cs], op=ALU.mult),
                     [by, bg], [by])
                k.op("pool", lambda e: e.tensor_tensor(out=y_t[:, cs:D], in0=y_t[:, cs:D], in1=g_t[:, cs:D], op=ALU.mult),
                     [by2, bg], [by2])

            def stage_b(tb):
                y_t, by = yt[tb % NBUF]
                by2 = yb2[tb % NBUF]
                for half in range(2):
                    p_t, bp = pt[(tb % NBUF) * 2 + half]
                    for c in range(8):
                        cc = half * 8 + c
                        k.op("pe", lambda e, c=c, cc=cc: e.transpose(out=p_t[:, c * 128:(c + 1) * 128],
                                                                    in_=y_t[:, cc * 128:(cc + 1) * 128],
                                                                    identity=self.ident[:]),
                             [by if cc < SPLIT else by2, self.b_ident], [bp], sig=(c == 7))
                    self.copy("act" if half == 0 else "dve", YG[:, half * 8:(half + 1) * 8, tb * 128:(tb + 1) * 128],
                              p_t[:].rearrange("p (c t) -> p c t", t=128), [bp], [bYG[half]])

            stage_a(0)
            for tb in range(NB):
                if tb + 1 < NB:
                    stage_a(tb + 1)
                stage_b(tb)
            k.barrier()

    def build(self):
        k = self.k
        self.setup()
        xsrc, bsrc = self.x, self.bX[0]
        for li in range(self.nlayers):
            j = li // 2
            xdst, bdst = self.XS[li % 2], self.bX[1 + li % 2]
            with ExitStack() as st:
                HT, bHT0 = self.sb([128, 16, S], BF16, "HT", st)
                bHT1 = k.buf("HTb")
                bHT1.persistent = False
                bHT = [bHT0, bHT1]
                c = self.gemm_ctx(st, resid=None)
                D_ = self.bD
                if li % 2 == 0:
                    c_sb, bc = self.sb([128, 4, S], BF16, "csb", st)
                    it = self.items_T(c, self.m_win_q[j], 16, 512, HT, bHT, "latnorm", self.CQ, D_["cq"], gam=self.m_qn[j])
                    it += self.items_T(c, self.m_win_kv[j], 16, 512, HT, bHT, "latnorm", self.CKV, D_["ckv"], gam=self.m_kvn[j])
                    it += self.items_T(c, self.m_win_kr[j], 16, 128, HT, bHT, "rope", self.KR, D_["kr"], tabi=0)
                    it += self.items_T(c, self.m_win_g[j], 16, 2048, HT, bHT, "silu", self.SG, D_["sg"])
                    it += self.items_T(c, self.m_uq_n[j], 4, 2048, c_sb, bc, "copy", self.QN, D_["qn"],
                                       pre=lambda: self.load_T(c_sb, bc, self.CQ, D_["cq"], 4))
                    it += self.items_T(c, self.m_uq_r[j], 4, 1024, c_sb, bc, "rope", self.QR, D_["qr"], tabi=0)
                    it += self.items_T(c, self.m_ukv_k[j], 4, 2048, c_sb, bc, "copy", self.KN, D_["kn"],
                                       pre=lambda: self.load_T(c_sb, bc, self.CKV, D_["ckv"], 4))
                    it += self.items_N(c, self.m_ukv_v[j], 4, 2048, c_sb, bc, "copy", self.V, D_["v"])
                else:
                    it = self.items_T(c, self.s_win[j][:, 0:2048], 16, 2048, HT, bHT, "rope", self.SQ, D_["sq"], tabi=1)
                    it += self.items_T(c, self.s_win[j][:, 2048:2304], 16, 256, HT, bHT, "rope", self.SK, D_["sk"], tabi=1)
                    it += self.items_N(c, self.s_win[j][:, 2304:2560], 16, 256, HT, bHT, "copy", self.SV, D_["sv"])
                    it += self.items_N(c, self.s_win[j][:, 2560:4608], 16, 2048, HT, bHT, "silu", self.SG, D_["sg"])
                it[0][0](c.wb[0])
                self.norm_stage(xsrc, bsrc, self.lnorm[li:li + 1, :], HT=HT, bHT=bHT)
                self.gemm_ctx(st, c=c)
                self.run_items(c, it, preloaded=True)
                k.barrier()
            with ExitStack() as st:
                YG, bYG0 = self.sb([128, 16, S], BF16, "YG", st)
                bYG1 = k.buf("YGb")
                bYG1.persistent = False
                bYG = [bYG0, bYG1]
                if li % 2 == 0:
                    self.mla_attn(YG, bYG0)
                else:
                    self.swa_attn(j)
                    self.gate_T(YG, bYG)
                c = self.gemm_ctx(st, resid=True)
                wout = self.m_wout[j] if li % 2 == 0 else self.s_wout[j]
                self.run_items(c, self.items_N(c, wout, 16, 2048, YG, bYG, "resid", xdst, bdst, xsrc=xsrc, bxsrc=bsrc))
                k.barrier()
            xsrc, bsrc = xdst, bdst
        self.norm_stage(xsrc, bsrc, self.fnorm, out=self.y, bout=self.bD["out"])
        k.wait_all("sp", [self.bD["out"]])
        k.barrier()
        self.es.close()


_PROG = {}


def _consts():
    cst = np.zeros((128, 8), np.float32)
    perm_mla = np.zeros((128, 128), np.float32)
    perm_swa = np.zeros((128, 128), np.float32)
    for i in range(128):
        d = i % 64
        jf = d % 32
        cst[i, 0] = np.float32(THETA) ** np.float32(-(2.0 * jf) / 64.0)
        cst[i, 1] = -1.0 if d < 32 else 1.0
        src = i + 32 if d < 32 else i - 32
        perm_mla[src, i] = 1.0
        if d < 16:
            cst[i, 2] = np.float32(THETA) ** np.float32(-(2.0 * (d % 8)) / 16.0)
            cst[i, 3] = -1.0 if d < 8 else 1.0
            src = i + 8 if d < 8 else i - 8
        else:
            cst[i, 2] = 0.0
            cst[i, 3] = 1.0
            src = i
        perm_swa[src, i] = 1.0
    ident = np.eye(128, dtype=np.float32)
    r = np.arange(128)[:, None]
    c = np.arange(128)[None, :]
    masks = np.zeros((128, 256), np.float32)
    masks[:, 0:128] = np.where(r > c, NEG, 0.0)
    masks[:, 128:256] = np.where(r > c, 0.0, NEG)
    return cst, perm_mla, perm_swa, ident, masks


def _weights(mla_w_in, mla_q_norm, mla_w_uq, mla_kv_norm, mla_w_ukv, mla_w_out, swa_w_in, swa_sinks, swa_w_out):
    c = np.ascontiguousarray
    kr = mla_w_in[:, :, 1024:1088]
    uq = mla_w_uq.reshape(2, 512, 16, 192)
    ukv = mla_w_ukv.reshape(2, 512, 16, 256)
    return {
        "m_win_q": c(mla_w_in[:, :, 0:512]),
        "m_win_kv": c(mla_w_in[:, :, 512:1024]),
        "m_win_kr": c(np.concatenate([kr, kr], axis=2)),
        "m_win_g": c(mla_w_in[:, :, 1088:3136]),
        "m_qn": c(mla_q_norm.reshape(2, 1, 512)),
        "m_kvn": c(mla_kv_norm.reshape(2, 1, 512)),
        "m_uq_n": c(uq[:, :, :, 0:128].reshape(2, 512, 2048)),
        "m_uq_r": c(uq[:, :, :, 128:192].reshape(2, 512, 1024)),
        "m_ukv_k": c(ukv[:, :, :, 0:128].reshape(2, 512, 2048)),
        "m_ukv_v": c(ukv[:, :, :, 128:256].reshape(2, 512, 2048)),
        "m_wout": c(mla_w_out),
        "s_win": c(swa_w_in),
        "s_sinks": c(swa_sinks.reshape(2, 1, 32)),
        "s_wout": c(swa_w_out),
    }


def kernel(x, positions, layer_norm, mla_w_in, mla_q_norm, mla_w_uq, mla_kv_norm, mla_w_ukv, mla_w_out,
           swa_w_in, swa_sinks, swa_w_out, final_norm, _nlayers=DEPTH, _stop=10 ** 9, _cores=8):
    f = lambda a: np.asarray(a, dtype=np.float32)
    x = f(x)
    positions = np.asarray(positions, dtype=np.int32)
    key = (_nlayers, _stop)
    if key not in _PROG:
        _PROG[key] = Prog(_nlayers, stop_after=_stop)
    prog = _PROG[key]
    cst, perm_mla, perm_swa, ident, masks = _consts()
    shared = _weights(f(mla_w_in), f(mla_q_norm), f(mla_w_uq), f(mla_kv_norm), f(mla_w_ukv), f(mla_w_out),
                      f(swa_w_in), f(swa_sinks), f(swa_w_out))
    shared.update({"lnorm": f(layer_norm), "fnorm": f(final_norm).reshape(1, D), "cst": cst, "perm_mla": perm_mla,
                   "perm_swa": perm_swa, "ident": ident, "masks": masks})
    in_maps = []
    for core in range(8):
        m = dict(shared)
        if core % 2 == 0:
            b = core // 2
            m["x"] = np.ascontiguousarray(x[b])
            m["pos"] = np.ascontiguousarray(positions[b].reshape(1, S))
        else:
            m["x"] = np.zeros((S, D), np.float32)
            m["pos"] = np.zeros((1, S), np.int32)
        in_maps.append(m)
    if _cores != 8:
        res = run_bass_kernel_spmd(prog.nc, in_maps[:_cores], core_ids=list(range(_cores)))
        return np.asarray(res.results[0]["y"])[None]
    res = run_bass_kernel_spmd(prog.nc, in_maps, core_ids=list(range(8)))
    out = np.stack([np.asarray(res.results[2 * b]["y"]) for b in range(4)], axis=0)
    return out.astype(np.float32)
```
